# Optimizing a Trainium2 kernel written in Bass

```python
import math
import numpy as np
import jax
import jax.numpy as jnp
from jax import lax

D_MODEL = 4096
BATCH = 4
SEQ = 2048
DEPTH = 2

CTX_LEN = 256
GRID_W = 64
N_MOD = 9
D_FF = 2 * D_MODEL
EPS = 1e-6

GDN_HEAD_DIM = 128
GDN_V_HEADS = (D_MODEL // 2) // GDN_HEAD_DIM
GDN_QK_HEADS = GDN_V_HEADS // 2
GDN_QK_DIM = GDN_QK_HEADS * GDN_HEAD_DIM
GDN_V_DIM = GDN_V_HEADS * GDN_HEAD_DIM
GDN_CONV = 5
GDN_CHUNK = 64

SC_DIM = D_MODEL // 4
SC_CONV = 3

POOL_WINDOWS = (2, 4, 8, 16)
POOL_DIM = D_MODEL // 4
POOL_GROUP = POOL_DIM // len(POOL_WINDOWS)

MIX_DIM = GDN_V_DIM + SC_DIM + POOL_DIM
PROJ_SIZES = (GDN_QK_DIM, GDN_QK_DIM, GDN_V_DIM, GDN_V_DIM, 4 * GDN_V_HEADS, SC_DIM, SC_DIM, SC_DIM, POOL_DIM)
D_IN = sum(PROJ_SIZES)

kernel_name = 'hybrid_gdn_shortconv_pool_dit_block'


def rmsnorm(x, w):
    xf = x.astype(jnp.float32)
    y = xf * lax.rsqrt(jnp.mean(xf * xf, axis=-1, keepdims=True) + EPS)
    return (y * w.astype(jnp.float32)).astype(x.dtype)


def l2norm(x):
    xf = x.astype(jnp.float32)
    return xf * lax.rsqrt(jnp.sum(xf * xf, axis=-1, keepdims=True) + EPS)


def adaln(x, w, shift, scale):
    return rmsnorm(x, w) * (1 + scale) + shift


def swiglu(h, w_gu, w_down):
    gate, up = jnp.split(h @ w_gu, 2, axis=-1)
    return (jax.nn.silu(gate) * up) @ w_down


def split_proj(p):
    return jnp.split(p, np.cumsum(PROJ_SIZES)[:-1].tolist(), axis=-1)


def depthwise_conv_centred(u, w, axis):
    k = w.shape[0]
    pad = k // 2
    n = u.shape[axis]
    pads = [(0, 0)] * u.ndim
    pads[axis] = (pad, pad)
    up = jnp.pad(u, pads)
    return sum(lax.slice_in_dim(up, j, j + n, axis=axis) * w[j] for j in range(k))


def centred_window_mean(u, window, axis):
    n = u.shape[axis]
    cs = jnp.cumsum(u.astype(jnp.float32), axis=axis)
    zero = jnp.zeros_like(lax.slice_in_dim(cs, 0, 1, axis=axis))
    cs = jnp.concatenate([zero, cs], axis=axis)
    t = jnp.arange(n)
    lo = jnp.clip(t - window // 2, 0, n)
    hi = jnp.clip(t - window // 2 + window, 0, n)
    total = jnp.take(cs, hi, axis=axis) - jnp.take(cs, lo, axis=axis)
    count = (hi - lo).astype(jnp.float32).reshape((n,) + (1,) * (u.ndim - axis - 1))
    return (total / count).astype(u.dtype)


def gated_delta_chunked(q, k, v, g, beta, s0):
    bsz, t_len, heads, _ = q.shape
    dv = v.shape[-1]
    n_chunks = t_len // GDN_CHUNK

    def chunked(a):
        a = jnp.moveaxis(a.astype(jnp.float32), 2, 1)
        return a.reshape((bsz, heads, n_chunks, GDN_CHUNK) + a.shape[3:])

    q, k, v, g, beta = (chunked(a) for a in (q, k, v, g, beta))
    gc = jnp.cumsum(g, axis=-1)
    incl = jnp.tril(jnp.ones((GDN_CHUNK, GDN_CHUNK), bool))
    strict = jnp.tril(jnp.ones((GDN_CHUNK, GDN_CHUNK), bool), -1)
    diff = gc[..., :, None] - gc[..., None, :]
    decay = jnp.where(incl, jnp.exp(jnp.where(incl, diff, 0.0)), 0.0)
    k_beta = k * beta[..., None]
    lmat = jnp.where(strict, jnp.einsum('bhnid,bhnjd->bhnij', k_beta, k) * decay, 0.0)
    eye = jnp.eye(GDN_CHUNK, dtype=jnp.float32)
    tmat = lax.linalg.triangular_solve(lmat + eye, jnp.broadcast_to(eye, lmat.shape),
                                       left_side=True, lower=True, unit_diagonal=True)
    u = jnp.einsum('bhnij,bhnje->bhnie', tmat, v * beta[..., None])
    w = jnp.einsum('bhnij,bhnjd->bhnid', tmat, k_beta * jnp.exp(gc)[..., None])
    attn = jnp.einsum('bhnid,bhnjd->bhnij', q, k) * decay
    q_dec = q * jnp.exp(gc)[..., None]
    g_last = gc[..., -1]
    k_dec = k * jnp.exp(g_last[..., None] - gc)[..., None]

    def step(state, xs):
        u_c, w_c, attn_c, q_c, k_c, gl_c = xs
        v_new = u_c - jnp.einsum('bhid,bhde->bhie', w_c, state)
        out = jnp.einsum('bhid,bhde->bhie', q_c, state) + jnp.einsum('bhij,bhje->bhie', attn_c, v_new)
        state = state * jnp.exp(gl_c)[..., None, None] + jnp.einsum('bhid,bhie->bhde', k_c, v_new)
        return state, out

    xs = tuple(jnp.moveaxis(a, 2, 0) for a in (u, w, attn, q_dec, k_dec, g_last))
    s_final, out = lax.scan(step, s0.astype(jnp.float32), xs)
    out = jnp.moveaxis(out, 0, 2).reshape(bsz, heads, t_len, dv)
    return jnp.moveaxis(out, 1, 2), s_final


def gdn_branch(q, k, v, z, ba, conv_w, a_log, dt_bias, norm_w, init_states):
    dtype = q.dtype
    bsz, t_len, _ = q.shape
    qkv = jax.nn.silu(depthwise_conv_centred(jnp.concatenate([q, k, v], axis=-1), conv_w, axis=1))
    q, k, v = jnp.split(qkv, [GDN_QK_DIM, 2 * GDN_QK_DIM], axis=-1)
    rep = GDN_V_HEADS // GDN_QK_HEADS
    q = jnp.repeat(l2norm(q.reshape(bsz, t_len, GDN_QK_HEADS, GDN_HEAD_DIM)) * GDN_HEAD_DIM ** -0.5, rep, axis=2)
    k = jnp.repeat(l2norm(k.reshape(bsz, t_len, GDN_QK_HEADS, GDN_HEAD_DIM)), rep, axis=2)
    v = v.reshape(bsz, t_len, GDN_V_HEADS, GDN_HEAD_DIM)
    ba = ba.astype(jnp.float32).reshape(bsz, t_len, 2, 2, GDN_V_HEADS)
    beta = jax.nn.sigmoid(ba[:, :, :, 0])
    g = -jnp.exp(a_log) * jax.nn.softplus(ba[:, :, :, 1] + dt_bias)
    if init_states is None:
        zero = jnp.zeros((bsz, GDN_V_HEADS, GDN_HEAD_DIM, GDN_HEAD_DIM), jnp.float32)
        init_states = (zero, zero)
    s_fwd0, s_bwd0 = init_states
    o_f, s_f = gated_delta_chunked(q, k, v, g[:, :, 0], beta[:, :, 0], s_fwd0)
    flip = lambda a: jnp.flip(a, axis=1)
    o_b, s_b = gated_delta_chunked(flip(q), flip(k), flip(v), flip(g[:, :, 1]), flip(beta[:, :, 1]), s_bwd0)
    o = o_f + flip(o_b)
    gate = jax.nn.silu(z.astype(jnp.float32).reshape(bsz, t_len, GDN_V_HEADS, GDN_HEAD_DIM))
    o = rmsnorm(o, norm_w) * gate
    return o.reshape(bsz, t_len, GDN_V_DIM).astype(dtype), (s_f, s_b)


def shortconv_branch(xin, b_gate, c_gate, conv_w, grid_rows):
    bsz, t_len, _ = xin.shape
    v = c_gate * xin
    if grid_rows is None:
        y = depthwise_conv_centred(v, conv_w, axis=1)
    else:
        y = depthwise_conv_centred(v.reshape(bsz, grid_rows, GRID_W, SC_DIM), conv_w, axis=2)
    return b_gate * y.reshape(bsz, t_len, SC_DIM)


def pool_branch(u, pool_w, pool_scale, grid_rows):
    bsz, t_len, _ = u.shape
    ug = u if grid_rows is None else u.reshape(bsz, grid_rows, GRID_W, POOL_DIM)
    groups = jnp.split(ug, len(POOL_WINDOWS), axis=-1)
    outs = [jnp.einsum('...c,cd->...d', centred_window_mean(gu, win, axis=1) - gu, pool_w[i])
            for i, (gu, win) in enumerate(zip(groups, POOL_WINDOWS))]
    y = jnp.concatenate(outs, axis=-1) * pool_scale
    return y.reshape(bsz, t_len, POOL_DIM)


def mixer_output(gdn_y, xin, b_gate, c_gate, pool_in, conv_short, pool_w, pool_scale, w_out, grid_rows):
    y = jnp.concatenate([gdn_y,
                         shortconv_branch(xin, b_gate, c_gate, conv_short, grid_rows),
                         pool_branch(pool_in, pool_w, pool_scale, grid_rows)], axis=-1)
    return y @ w_out


def setup_inputs(seed: int = 0) -> dict:
    key = jax.random.key(seed)
    ks = jax.random.split(key, 24)
    f32 = jnp.float32
    L, D = DEPTH, D_MODEL

    def normal(k, shape, scale=1.0):
        return jax.random.normal(k, shape, f32) * scale

    def gain(k, shape):
        return 1.0 + 0.1 * jax.random.normal(k, shape, f32)

    dt = jnp.exp(jax.random.uniform(ks[12], (L, 2, GDN_V_HEADS), f32, math.log(1e-3), math.log(1e-1)))
    return {
        'x': normal(ks[0], (BATCH, SEQ, D)),
        'c': normal(ks[1], (BATCH, D)),
        'ctx': normal(ks[2], (BATCH, CTX_LEN, D)),
        'c_ctx': normal(ks[3], (D,)),
        'w_mod': normal(ks[4], (L, D, N_MOD * D), 0.5 * D ** -0.5),
        'b_mod': normal(ks[5], (L, N_MOD * D), 0.02),
        'norm_ffn1': gain(ks[6], (L, D)),
        'w_ffn1_gu': normal(ks[7], (L, D, 2 * D_FF), D ** -0.5),
        'w_ffn1_down': normal(ks[8], (L, D_FF, D), D_FF ** -0.5),
        'norm_mix': gain(ks[9], (L, D)),
        'w_in': normal(ks[10], (L, D, D_IN), D ** -0.5),
        'conv_qkv': normal(ks[11], (L, GDN_CONV, 2 * GDN_QK_DIM + GDN_V_DIM), GDN_CONV ** -0.5),
        'a_log': jnp.log(jax.random.uniform(ks[13], (L, 2, GDN_V_HEADS), f32, 1.0, 16.0)),
        'dt_bias': dt + jnp.log(-jnp.expm1(-dt)),
        'gdn_norm': gain(ks[14], (L, GDN_HEAD_DIM)),
        'conv_short': normal(ks[15], (L, SC_CONV, SC_DIM), SC_CONV ** -0.5),
        'pool_w': normal(ks[16], (L, len(POOL_WINDOWS), POOL_GROUP, POOL_GROUP), POOL_GROUP ** -0.5),
        'pool_scale': gain(ks[17], (L, POOL_DIM)),
        'w_out': normal(ks[18], (L, MIX_DIM, D), MIX_DIM ** -0.5),
        'norm_ffn2': gain(ks[19], (L, D)),
        'w_ffn2_gu': normal(ks[20], (L, D, 2 * D_FF), D ** -0.5),
        'w_ffn2_down': normal(ks[21], (L, D_FF, D), D_FF ** -0.5),
        'norm_final': gain(ks[22], (D,)),
    }


def reference(x, c, ctx, c_ctx, w_mod, b_mod, norm_ffn1, w_ffn1_gu, w_ffn1_down, norm_mix, w_in,
              conv_qkv, a_log, dt_bias, gdn_norm, conv_short, pool_w, pool_scale, w_out,
              norm_ffn2, w_ffn2_gu, w_ffn2_down, norm_final):
    rows = x.shape[1] // GRID_W
    for l in range(DEPTH):
        last = l == DEPTH - 1
        mod_x = (jax.nn.silu(c) @ w_mod[l] + b_mod[l]).reshape(-1, N_MOD, 1, D_MODEL)
        mod_c = (jax.nn.silu(c_ctx) @ w_mod[l] + b_mod[l]).reshape(N_MOD, D_MODEL)

        x = x + 0.5 * mod_x[:, 2] * swiglu(adaln(x, norm_ffn1[l], mod_x[:, 0], mod_x[:, 1]), w_ffn1_gu[l], w_ffn1_down[l])
        ctx = ctx + 0.5 * mod_c[2] * swiglu(adaln(ctx, norm_ffn1[l], mod_c[0], mod_c[1]), w_ffn1_gu[l], w_ffn1_down[l])

        pc = split_proj(adaln(ctx, norm_mix[l], mod_c[3], mod_c[4]) @ w_in[l])
        px = split_proj(adaln(x, norm_mix[l], mod_x[:, 3], mod_x[:, 4]) @ w_in[l])
        gdn_c, states_c = gdn_branch(pc[0], pc[1], pc[2], pc[3], pc[4], conv_qkv[l], a_log[l], dt_bias[l],
                                     gdn_norm[l], None)
        gdn_x, _ = gdn_branch(px[0], px[1], px[2], px[3], px[4], conv_qkv[l], a_log[l], dt_bias[l],
                              gdn_norm[l], states_c)
        x = x + mod_x[:, 5] * mixer_output(gdn_x, px[5], px[6], px[7], px[8], conv_short[l], pool_w[l],
                                           pool_scale[l], w_out[l], rows)
        if not last:
            ctx = ctx + mod_c[5] * mixer_output(gdn_c, pc[5], pc[6], pc[7], pc[8], conv_short[l], pool_w[l],
                                                pool_scale[l], w_out[l], None)
            ctx = ctx + 0.5 * mod_c[8] * swiglu(adaln(ctx, norm_ffn2[l], mod_c[6], mod_c[7]), w_ffn2_gu[l], w_ffn2_down[l])

        x = x + 0.5 * mod_x[:, 8] * swiglu(adaln(x, norm_ffn2[l], mod_x[:, 6], mod_x[:, 7]), w_ffn2_gu[l], w_ffn2_down[l])
    return rmsnorm(x, norm_final)
```

```python
import numpy as np
import concourse.bass as bass
import concourse.mybir as mybir
from concourse.bass_utils import run_bass_kernel_spmd

F32 = mybir.dt.float32
BF16 = mybir.dt.bfloat16
AF = mybir.ActivationFunctionType
ALU = mybir.AluOpType
AX = mybir.AxisListType

SAME_ENGINE_SYNC = True
N_DMA_SEMS = 24


class Res:
    __slots__ = ("name", "w", "r", "excl")

    def __init__(self, name=""):
        self.name = name
        self.w = None
        self.r = {}
        self.excl = False


class V:
    __slots__ = ("ap", "res")

    def __init__(self, ap, res):
        self.ap = ap
        self.res = list(res) if isinstance(res, (list, tuple)) else [res]


class TT:
    def __init__(self, t, name="", nres=1):
        self.t = t
        self.res = [Res(f"{name}{i}") for i in range(nres)]

    def __getitem__(self, idx):
        return V(self.t[idx], self.res)

    def v(self, ap, ri=None):
        if ri is None:
            return V(ap, self.res)
        if isinstance(ri, int):
            return V(ap, [self.res[ri]])
        return V(ap, [self.res[i] for i in ri])


class PT(TT):
    def __init__(self, t, name="", nres=1):
        super().__init__(t, name, nres)
        for r in self.res:
            r.excl = True


class Sched:
    def __init__(self, nc):
        self.nc = nc
        self.engs = {"pe": nc.tensor, "act": nc.scalar, "dve": nc.vector, "pool": nc.gpsimd, "sp": nc.sync}
        self.sems = {}
        self.cnt = {}
        for e in self.engs:
            self.sems[e] = nc.alloc_semaphore(name=f"c_{e}")
            self.cnt[e] = 0
        self.seen = {e: {} for e in self.engs}
        self.pending = {e: [] for e in self.engs}
        self.dma_sems = {}
        self.dma_rr = {}
        for q in ("sp", "act", "pool"):
            keys = []
            for i in range(N_DMA_SEMS):
                k = f"d_{q}{i}"
                self.sems[k] = nc.alloc_semaphore(name=k)
                self.cnt[k] = 0
                keys.append(k)
            self.dma_sems[q] = keys
            self.dma_rr[q] = 0
        self.n_wait = 0
        self.n_ins = 0

    def _wait(self, e, key, val):
        if val <= 0:
            return
        if key == e and not SAME_ENGINE_SYNC:
            return
        if self.seen[e].get(key, 0) >= val:
            return
        self.engs[e].wait_ge(self.sems[key], val)
        self.seen[e][key] = val
        self.n_wait += 1

    def _deps(self, e, reads, writes):
        needs = {}
        for v in reads:
            for r in v.res:
                if r.w is not None:
                    if r.w[0] == "pending":
                        raise RuntimeError(f"read of unsignalled resource {r.name}")
                    needs[r.w[0]] = max(needs.get(r.w[0], 0), r.w[1])
                if r.excl:
                    for k, val in r.r.items():
                        if k != "pending" and k != e:
                            needs[k] = max(needs.get(k, 0), val)
        for v in writes:
            for r in v.res:
                if r.w is not None:
                    if r.w[0] == "pending":
                        if r.w[1] != e:
                            raise RuntimeError(f"write of unsignalled resource {r.name}")
                    else:
                        needs[r.w[0]] = max(needs.get(r.w[0], 0), r.w[1])
                for k, val in r.r.items():
                    if k == "pending":
                        if val != e:
                            raise RuntimeError(f"WAR on unsignalled resource {r.name}")
                        continue
                    needs[k] = max(needs.get(k, 0), val)
        for k, val in needs.items():
            self._wait(e, k, val)

    def _mark(self, key, val, reads, writes):
        for v in reads:
            for r in v.res:
                r.r.pop("pending", None)
                r.r[key] = max(r.r.get(key, 0), val)
        for v in writes:
            for r in v.res:
                r.w = (key, val)
                r.r = {}

    def issue(self, e, reads, writes, emit, sig=True):
        reads = [v for v in reads if isinstance(v, V)]
        writes = [v for v in writes if isinstance(v, V)]
        self._deps(e, reads, writes)
        ins = emit()
        self.n_ins += 1
        if sig:
            self.cnt[e] += 1
            ins.then_inc(self.sems[e], 1)
            val = self.cnt[e]
            for (rd, wr) in self.pending[e]:
                self._mark(e, val, rd, wr)
            self.pending[e] = []
            self._mark(e, val, reads, writes)
        else:
            self.pending[e].append((reads, writes))
            for v in reads:
                for r in v.res:
                    r.r["pending"] = e
            for v in writes:
                for r in v.res:
                    r.w = ("pending", e)
                    r.r = {}
        return ins

    def dma(self, q, out, in_, **kw):
        e = q
        self._deps(e, [in_], [out])
        keys = self.dma_sems[q]
        k = keys[self.dma_rr[q] % len(keys)]
        self.dma_rr[q] += 1
        self._wait(e, k, self.cnt[k])
        ins = self.engs[e].dma_start(out=out.ap, in_=in_.ap, **kw)
        self.cnt[k] += 16
        ins.then_inc(self.sems[k], 16)
        self.n_ins += 1
        self._mark(k, self.cnt[k], [in_], [out])
        return ins

    def barrier(self):
        for e in self.engs:
            assert not self.pending[e], f"pending on {e} at barrier"
        for e in self.engs:
            for k, val in self.cnt.items():
                if k != e:
                    self._wait(e, k, val)

    def finish(self):
        for e in self.engs:
            assert not self.pending[e]
        for k, val in self.cnt.items():
            if k != "sp":
                self._wait("sp", k, val)

    @staticmethod
    def _a(x):
        return x.ap if isinstance(x, V) else x

    def act(self, out, in_, func, bias=0.0, scale=1.0, accum=None, e="act"):
        kw = {}
        if accum is not None:
            kw["accum_out"] = accum.ap
        return self.issue(e, [in_, bias, scale], [out, accum],
                          lambda: self.engs[e].activation(out=out.ap, in_=in_.ap, func=func, bias=self._a(bias),
                                                          scale=self._a(scale), **kw))

    def tt(self, e, out, in0, in1, op):
        return self.issue(e, [in0, in1], [out],
                          lambda: self.engs[e].tensor_tensor(out=out.ap, in0=in0.ap, in1=in1.ap, op=op))

    def ts(self, e, out, in0, s1, s2, op0, op1=None, accum=None):
        kw = {}
        if op1 is not None:
            kw["op1"] = op1
        if accum is not None:
            kw["accum_out"] = accum.ap
        return self.issue(e, [in0, s1, s2], [out, accum],
                          lambda: self.engs[e].tensor_scalar(out=out.ap, in0=in0.ap, scalar1=self._a(s1),
                                                             scalar2=self._a(s2), op0=op0, **kw))

    def stt(self, e, out, in0, scalar, in1, op0, op1):
        return self.issue(e, [in0, scalar, in1], [out],
                          lambda: self.engs[e].scalar_tensor_tensor(out=out.ap, in0=in0.ap, scalar=self._a(scalar),
                                                                    in1=in1.ap, op0=op0, op1=op1))

    def copy(self, e, out, in_):
        if e == "act":
            return self.issue(e, [in_], [out], lambda: self.engs[e].copy(out=out.ap, in_=in_.ap))
        return self.issue(e, [in_], [out], lambda: self.engs[e].tensor_copy(out=out.ap, in_=in_.ap))

    def memset(self, e, out, val):
        return self.issue(e, [], [out], lambda: self.engs[e].memset(out.ap, val))

    def mm(self, out, lhsT, rhs, start=True, stop=True, sig=None):
        if sig is None:
            sig = stop
        return self.issue("pe", [lhsT, rhs], [out],
                          lambda: self.engs["pe"].matmul(out.ap, lhsT.ap, rhs.ap, start=start, stop=stop), sig=sig)

    def transpose(self, out, in_, ident, sig=True):
        return self.issue("pe", [in_, ident], [out],
                          lambda: self.engs["pe"].transpose(out.ap, in_.ap, ident.ap), sig=sig)


D = 4096
NCH = 32
DFF = 8192
NCTX = 256
NXT = 2048
NTOK = NCTX + NXT
NTILE = NTOK // 128
DEPTH = 2
NMOD = 9
EPS = 1e-6
D_IN = 10304
BLOCKS = [(0, 256), (256, 512), (768, 512), (1280, 512), (1792, 512)]


class Ctx:
    pass


_UID = [0]


def sbt(nc, name, shape, dt):
    _UID[0] += 1
    return nc.sbuf_tensor(f"{name}_{_UID[0]}", shape, dt)


def pst(nc, name, shape, dt):
    _UID[0] += 1
    return nc.psum_tensor(f"{name}_{_UID[0]}", shape, dt)


def dram_in(nc, name, shape, dt=F32):
    return nc.dram_tensor(name, list(shape), dt, kind="ExternalInput").ap()


def build_program(cfg):
    nc = bass.Bass("TRN2", target_bir_lowering=False)
    S = Sched(nc)
    K = Ctx()
    K.nc, K.S, K.cfg = nc, S, cfg
    K.xin = dram_in(nc, "xin", [NTOK, D])
    K.cv = dram_in(nc, "cv", [128, 64])
    K.wshape = {"w_mod": [D, NMOD * D], "w_ffn1_gu": [D, 2 * DFF], "w_ffn2_gu": [D, 2 * DFF],
                "w_ffn1_down": [DFF, D], "w_ffn2_down": [DFF, D], "w_in": [D, D_IN], "w_out": [D, D],
                "pool_w": [4 * 256, 256]}
    K.wts = {}
    K.bm = dram_in(nc, "bm", [128, DEPTH * 288])
    K.nrm = dram_in(nc, "nrm", [128, DEPTH * 3 * 32])
    K.identd = dram_in(nc, "ident", [128, 128])
    K.nfin = dram_in(nc, "nfin", [128, D])
    K.out = nc.dram_tensor("out", [NXT, D], F32, kind="ExternalOutput").ap()
    K.xres = nc.dram_tensor("xres", [NTOK, D], F32).ap()
    K.gvec = nc.dram_tensor("gvec", [DEPTH * 3 * 2 * 32, 128], F32).ap()
    K.xres_res = [[Res(f"xres{t}_{f}") for f in range(16)] for t in range(NTILE)]
    K.xin_res = Res("xin")
    K.out_res = Res("out")
    K.gvec_res = Res("gvec")
    K.ext = Res("ext")
    K.ident = TT(nc.alloc_sbuf_tensor("identt", [128, 128], F32), "ident")
    K.ones = TT(nc.alloc_sbuf_tensor("onest", [128, 128], F32), "ones")
    K.modT = TT(nc.alloc_sbuf_tensor("modT", [128, DEPTH, 288, 2], F32), "modT")
    K.Amod = TT(nc.alloc_sbuf_tensor("Amod", [128, DEPTH * 3 * 2, 32], F32), "Amod")
    K.nrmt = TT(nc.alloc_sbuf_tensor("nrmt", [128, DEPTH * 3, 32], F32), "nrmt")
    S.dma("sp", K.ident[:], V(K.identd, K.ext))
    S.dma("sp", V(K.nrmt.t[:].rearrange("p a b -> p (a b)"), K.nrmt.res), V(K.nrm, K.ext))
    S.memset("dve", K.ones[:], 1.0)
    K.consts = TT(nc.alloc_sbuf_tensor("constst", [128, 4], F32), "consts")
    S.memset("dve", K.consts[:, 0:1], EPS)
    S.memset("dve", K.consts[:, 1:2], 1.0)
    S.memset("dve", K.consts[:, 2:3], 0.0)
    mixer_setup(K)
    if not cfg.get("no_mod"):
        phase_mod(K)
    S.barrier()
    if cfg.get("dump_mod"):
        S.dma("sp", V(K.out[0:128, 0:DEPTH * 288 * 2], K.out_res),
              V(K.modT.t[:].rearrange("p a b c -> p (a b c)"), K.modT.res))
        S.dma("sp", V(K.out[128:256, 0:DEPTH * 3 * 2 * 32], K.out_res),
              V(K.Amod.t[:].rearrange("p a b -> p (a b)"), K.Amod.res))
        S.dma("sp", V(K.out[256:256 + DEPTH * 3 * 2 * 32, 0:128], K.out_res), V(K.gvec, K.gvec_res))
        S.finish()
        nc.used_inputs = used_inputs(K)
        return nc
    for l in range(DEPTH):
        if not cfg.get("no_ffn1"):
            phase_ffn(K, l, 0)
            S.barrier()
        if cfg.get("stop") == ("ffn1", l):
            break
        phase_mixer(K, l)
        if cfg.get("stop") == ("mix", l):
            break
        phase_ffn(K, l, 2)
        S.barrier()
    if cfg.get("no_dump"):
        pass
    elif cfg.get("dump_yT"):
        dump_yT(K)
    elif cfg.get("dump_xres"):
        dump_xres(K)
    else:
        phase_final(K)
    S.finish()
    K.stats = (S.n_ins, S.n_wait)
    nc.used_inputs = used_inputs(K)
    return nc


def used_inputs(K):
    return ["xin", "cv", "bm", "nrm", "ident", "nfin"] + list(K.wts.keys()) + list(getattr(K, "extra_inputs", []))


def phase_mod(K):
    nc, S = K.nc, K.S
    from contextlib import ExitStack
    with ExitStack() as st:
        cvt = TT(st.enter_context(sbt(nc, "cvt", [128, 64], F32)), "cvt")
        sT = TT(st.enter_context(sbt(nc, "sT", [128, 32, 2], BF16)), "sT")
        bmt = TT(st.enter_context(sbt(nc, "bmt", [128, DEPTH, 288], F32)), "bmt")
        wm = [TT(st.enter_context(sbt(nc, f"wm{i}", [128, 32, 512], BF16)), f"wm{i}") for i in range(3)]
        pm = [PT(st.enter_context(pst(nc, f"pm{i}", [128, 8], F32)), f"pm{i}") for i in range(2)]
        gT = TT(st.enter_context(sbt(nc, "gT", [128, 32], F32)), "gT")
        gR = TT(st.enter_context(sbt(nc, "gR", [32, 128], F32)), "gR")
        pg = PT(st.enter_context(pst(nc, "pgT", [32, 128], F32)), "pgT")
        S.dma("sp", cvt[:], V(K.cv, K.ext))
        S.dma("sp", V(bmt.t[:].rearrange("p a b -> p (a b)"), bmt.res), V(K.bm, K.ext))
        S.act(V(sT.t[:].rearrange("p a b -> p (a b)"), sT.res), cvt[:], AF.Silu)
        nblk = K.cfg.get("mod_blocks", 72)
        if K.cfg.get("skip_mod"):
            nblk = 0
            S.memset("dve", V(K.modT.t[:].rearrange("p a b c -> p (a b c)"), K.modT.res), 0.25)
        i = 0
        for l in range(DEPTH):
            for blk in range(nblk):
                w = wm[i % 3]
                p = pm[i % 2]
                i += 1
                src = W(K, "w_mod", l)[:, blk * 512:(blk + 1) * 512].rearrange("(kc p) n -> p kc n", p=128)
                wload(K, w.t[:, :, :], w.res, src, K.cfg.get("wsplit", 4))
                for q in range(4):
                    for kc in range(32):
                        S.mm(p[:, q * 2:(q + 1) * 2], w[:, kc, q * 128:(q + 1) * 128], sT[:, kc, :],
                             start=(kc == 0), stop=(kc == 31))
                for r in range(2):
                    S.tt("dve", K.modT[:, l, blk * 4:(blk + 1) * 4, r], p[:, r:8:2], bmt[:, l, blk * 4:(blk + 1) * 4],
                         ALU.add)
        for l in range(DEPTH if not K.cfg.get("no_derived") else 0):
            for s in range(3):
                for r in range(2):
                    idx = (l * 3 + s) * 2 + r
                    S.stt("dve", K.Amod[:, idx, :], K.modT[:, l, (3 * s + 1) * 32:(3 * s + 2) * 32, r], 1.0,
                          K.nrmt[:, l * 3 + s, :], ALU.add, ALU.mult)
                    S.ts("dve", gT[:], K.modT[:, l, (3 * s + 2) * 32:(3 * s + 3) * 32, r],
                         0.5 if s != 1 else 1.0, None, ALU.mult)
                    S.transpose(pg[:], gT[:], K.ident[:])
                    S.copy("dve", gR[:], pg[:])
                    S.dma("sp", V(K.gvec[idx * 32:(idx + 1) * 32, :], K.gvec_res), gR[:])


def W(K, name, l):
    key = f"{name}_{l}"
    if key not in K.wts:
        K.wts[key] = dram_in(K.nc, key, K.wshape[name])
    return K.wts[key]


def wload(K, dst3, res, src3, nsplit):
    k = dst3.shape[1]
    step = k // nsplit
    for i in range(nsplit):
        K.S.dma("pool", V(dst3[:, i * step:(i + 1) * step, :], res), V(src3[:, i * step:(i + 1) * step, :], K.ext))


def shiftv(K, l, s, r, c):
    return K.modT[:, l, 3 * s * 32 + c, r:r + 1]


def phase_ffn(K, l, s):
    nc, S = K.nc, K.S
    from contextlib import ExitStack
    last = (l == DEPTH - 1)
    parts = K.cfg.get("ffn_parts", "fgrd")
    w_gu = W(K, "w_ffn1_gu" if s == 0 else "w_ffn2_gu", l) if "g" in parts else None
    w_dn = W(K, "w_ffn1_down" if s == 0 else "w_ffn2_down", l) if "d" in parts else None
    with ExitStack() as st:
        regA = st.enter_context(sbt(nc, "regA", [128, 32 * 512], BF16))
        regB = st.enter_context(sbt(nc, "regB", [128, 64 * 512], BF16))
        resA = [Res(f"A{c}") for c in range(32)]
        resB = [Res(f"B{c}") for c in range(64)]
        wp = [TT(st.enter_context(sbt(nc, f"wp{i}", [128, 32 * 512], BF16)), f"wp{i}") for i in range(2)]
        sg = [TT(st.enter_context(sbt(nc, f"sg{i}", [128, 512], F32)), f"sg{i}") for i in range(2)]
        slab = [TT(st.enter_context(sbt(nc, f"slab{i}", [128, 256], F32)), f"slab{i}") for i in range(3)]
        oslab = [TT(st.enter_context(sbt(nc, f"oslab{i}", [128, 256], F32)), f"oslab{i}") for i in range(3)]
        ss = TT(st.enter_context(sbt(nc, "ss", [128, 4], F32)), "ss")
        diag = [TT(st.enter_context(sbt(nc, f"diag{i}", [128, 128], F32)), f"diag{i}") for i in range(2)]
        pgu = [PT(st.enter_context(pst(nc, f"pgu{i}", [128, 512], F32)), f"pgu{i}") for i in range(4)]
        py = [PT(st.enter_context(pst(nc, f"py{i}", [128, 512], F32)), f"py{i}") for i in range(2)]
        pt = [PT(st.enter_context(pst(nc, f"pt{i}", [128, 512], F32)), f"pt{i}") for i in range(2)]
        L = Ctx()
        L.regA, L.regB, L.resA, L.resB, L.ss, L.diag, L.pt = regA, regB, resA, resB, ss, diag, pt
        L.wp, L.py, L.slab, L.oslab = wp, py, slab, oslab
        L.wi, L.yi, L.si = 0, 0, 0
        gi = 0
        for bi, (t0, T) in enumerate(BLOCKS):
            r = 0 if bi == 0 else 1
            if bi == 0 and last and s == 2:
                continue
            if K.cfg.get("only_blocks") is not None and bi not in K.cfg["only_blocks"]:
                continue
            first_pass = (l == 0 and s == 0)
            src = K.xin if first_pass else K.xres
            front_end(K, L, src, first_pass, t0, T, l, s, r)
            nt = T // 128
            xnT = regA[:, :].rearrange("p (c t) -> p c t", c=32)
            hT = regB[:, :].rearrange("p (c t) -> p c t", c=64)
            parts = K.cfg.get("ffn_parts", "fgrd")
            for jp in range(32 if "g" in parts else 0):
                w = wp[L.wi % 2]
                L.wi += 1
                wv = w.t[:, :].rearrange("p (k n) -> p k n", k=32)
                for half in range(2):
                    src_w = w_gu[:, half * DFF + jp * 256: half * DFF + (jp + 1) * 256].rearrange(
                        "(kc p) n -> p kc n", p=128)
                    wload(K, wv[:, :, half * 256:(half + 1) * 256], w.res, src_w, K.cfg.get("wsplit", 4))
                for jj in range(2):
                    j = jp * 2 + jj
                    pg_ = pgu[(gi % 2) * 2]
                    pu_ = pgu[(gi % 2) * 2 + 1]
                    sg_ = sg[gi % 2]
                    gi += 1
                    for kc in range(32):
                        S.mm(pg_[:, :T], V(wv[:, kc, jj * 128:(jj + 1) * 128], w.res),
                             V(xnT[:, kc, :T], resA[kc]), start=(kc == 0), stop=(kc == 31))
                    for kc in range(32):
                        S.mm(pu_[:, :T], V(wv[:, kc, 256 + jj * 128:256 + (jj + 1) * 128], w.res),
                             V(xnT[:, kc, :T], resA[kc]), start=(kc == 0), stop=(kc == 31))
                    S.act(sg_[:, :T], pg_[:, :T], AF.Silu)
                    S.tt("dve", V(hT[:, j, :T], resB[j]), sg_[:, :T], pu_[:, :T], ALU.mult)
            if "d" in parts:
                down_stage(K, L, l, s, r, t0, T, hT, resB, 64, w_dn, first_pass)


def down_stage(K, L, l, s, r, t0, T, hT, resH, njc, w_dn, first_pass):
    S = K.S
    nt = T // 128
    idx = (l * 3 + s) * 2 + r
    grow = L.regA[:, 0:8192].bitcast(F32)
    gsrc = K.gvec[idx * 32:(idx + 1) * 32, :].rearrange("a b -> (a b)").partition_broadcast(128)
    S.dma("sp", V(grow, L.resA), V(gsrc, K.gvec_res))
    for fb in range(16):
        w = L.wp[L.wi % 2]
        L.wi += 1
        wv = w.t[:, 0:njc * 256].rearrange("p (k n) -> p k n", k=njc)
        nh = njc // 32
        for half in range(nh):
            src_w = w_dn[half * 4096:(half + 1) * 4096, fb * 256:(fb + 1) * 256].rearrange(
                "(jc p) n -> p jc n", p=128)
            wload(K, wv[:, half * 32:(half + 1) * 32, :], w.res, src_w, K.cfg.get("wsplit", 4))
        for ti in range(nt):
            tile_i = t0 // 128 + ti
            p_ = L.py[L.yi % 2]
            L.yi += 1
            for jc in range(njc):
                S.mm(p_[:, :256], V(hT[:, jc, ti * 128:(ti + 1) * 128], resH[jc]), V(wv[:, jc, :], w.res),
                     start=(jc == 0), stop=(jc == njc - 1))
            sl = L.slab[L.si % 3]
            osl = L.oslab[L.si % 3]
            L.si += 1
            rows = slice(t0 + ti * 128, t0 + (ti + 1) * 128)
            cols = slice(fb * 256, (fb + 1) * 256)
            if first_pass:
                S.dma("sp", sl[:], V(K.xin[rows, cols], K.xin_res))
            else:
                S.dma("sp", sl[:], V(K.xres[rows, cols], K.xres_res[tile_i][fb]))
            S.tt("dve", osl[:], p_[:, :256], V(grow[:, cols], L.resA), ALU.mult)
            S.tt("pool", osl[:], osl[:], sl[:], ALU.add)
            S.dma("act", V(K.xres[rows, cols], K.xres_res[tile_i][fb]), osl[:])


def front_end(K, L, src, first_pass, t0, T, l, s, r):
    S = K.S
    nt = T // 128
    xnT = L.regA[:, :].rearrange("p (c t) -> p c t", c=32)
    idx = (l * 3 + s) * 2 + r
    for ti in range(nt):
        tile_i = t0 // 128 + ti
        xt = L.regB[:, (ti % 2) * 8192:(ti % 2 + 1) * 8192].bitcast(F32)
        xt_res = L.resB[(ti % 2) * 16:(ti % 2 + 1) * 16]
        junk = L.regB[:, 16384:16384 + 4096]
        junk_res = L.resB[32:40]
        rows = slice(t0 + ti * 128, t0 + (ti + 1) * 128)
        if first_pass:
            S.dma("sp", V(xt, xt_res), V(src[rows, :], K.xin_res))
        else:
            S.dma("sp", V(xt, xt_res), V(src[rows, :], K.xres_res[tile_i]))
        S.act(V(junk, junk_res), V(xt, xt_res), AF.Square, accum=L.ss[:, 0:1])
        S.act(L.ss[:, 1:2], L.ss[:, 0:1], AF.Ln, bias=K.consts[:, 0:1], scale=1.0 / D)
        S.act(L.ss[:, 2:3], L.ss[:, 1:2], AF.Exp, scale=-0.5)
        dg = L.diag[ti % 2]
        S.ts("dve", dg[:], K.ident[:], L.ss[:, 2:3], None, ALU.mult)
        for c4 in range(8):
            p_ = L.pt[c4 % 2]
            for q in range(4):
                c = c4 * 4 + q
                S.mm(p_[:, q * 128:(q + 1) * 128], V(xt[:, c * 128:(c + 1) * 128], xt_res), dg[:],
                     start=True, stop=True, sig=(q == 3))
            for q in range(4):
                c = c4 * 4 + q
                o = V(xnT[:, c, ti * 128:(ti + 1) * 128], L.resA[c])
                S.ts("dve", o, p_[:, q * 128:(q + 1) * 128], K.Amod[:, idx, c:c + 1], shiftv(K, l, s, r, c),
                     ALU.mult, ALU.add)


def dump_xres(K):
    nc, S = K.nc, K.S
    bb = [TT(nc.alloc_sbuf_tensor(f"dumpb{i}", [128, D], F32), f"dumpb{i}") for i in range(2)]
    for t in range(NXT // 128):
        b = bb[t % 2]
        S.dma("sp", b[:], V(K.xres[NCTX + t * 128:NCTX + (t + 1) * 128, :], K.xres_res[2 + t]))
        S.dma("sp", V(K.out[t * 128:(t + 1) * 128, :], K.out_res), b[:])


def dump_yT(K):
    nc, S = K.nc, K.S
    yb = [TT(nc.alloc_sbuf_tensor(f"dyb{i}", [128, NXT], BF16), f"dyb{i}") for i in range(2)]
    yf = [TT(nc.alloc_sbuf_tensor(f"dyf{i}", [128, NXT], F32), f"dyf{i}") for i in range(2)]
    o2 = K.out.rearrange("a (b t) -> (a b) t", t=NXT)
    which = K.cfg["dump_yT"]
    for c in range(32):
        b, f = yb[c % 2], yf[c % 2]
        if which == "x":
            S.dma("sp", b[:], V(K.yT[c * 128:(c + 1) * 128, NCTX:NTOK], K.yT_res[c]))
            S.copy("dve", f[:], b[:])
            S.dma("sp", V(o2[c * 128:(c + 1) * 128, :], K.out_res), f[:])
        else:
            S.dma("sp", b[:, 0:NCTX], V(K.yT[c * 128:(c + 1) * 128, 0:NCTX], K.yT_res[c]))
            S.copy("dve", f[:, 0:NCTX], b[:, 0:NCTX])
            S.dma("sp", V(o2[c * 128:(c + 1) * 128, 0:NCTX], K.out_res), f[:, 0:NCTX])


def phase_final(K):
    nc, S = K.nc, K.S
    from contextlib import ExitStack
    with ExitStack() as st:
        nf = TT(st.enter_context(sbt(nc, "nf", [128, D], F32)), "nf")
        xt = [TT(st.enter_context(sbt(nc, f"fx{i}", [128, D], F32)), f"fx{i}") for i in range(2)]
        yo = [TT(st.enter_context(sbt(nc, f"fy{i}", [128, D], F32)), f"fy{i}") for i in range(2)]
        junk = TT(st.enter_context(sbt(nc, "fjunk", [128, D], BF16)), "fjunk")
        ss = TT(st.enter_context(sbt(nc, "fss", [128, 4], F32)), "fss")
        S.dma("sp", nf[:], V(K.nfin, K.ext))
        for t in range(NXT // 128):
            x_ = xt[t % 2]
            y_ = yo[t % 2]
            tile_i = 2 + t
            S.dma("sp", x_[:], V(K.xres[NCTX + t * 128:NCTX + (t + 1) * 128, :], K.xres_res[tile_i]))
            S.act(junk[:], x_[:], AF.Square, accum=ss[:, 0:1])
            S.act(ss[:, 1:2], ss[:, 0:1], AF.Ln, bias=K.consts[:, 0:1], scale=1.0 / D)
            S.act(ss[:, 2:3], ss[:, 1:2], AF.Exp, scale=-0.5)
            S.stt("dve", y_[:], x_[:], ss[:, 2:3], nf[:], ALU.mult, ALU.mult)
            S.dma("act", V(K.out[t * 128:(t + 1) * 128, :], K.out_res), y_[:])


def fm(v):
    v = np.asarray(v, np.float32)
    return np.ascontiguousarray(v.reshape(-1, 128).T)


def host_inputs(inp, b):
    m = {}
    m["xin"] = np.ascontiguousarray(np.concatenate([inp["ctx"][b], inp["x"][b]], axis=0), dtype=np.float32)
    cv = np.stack([fm(inp["c_ctx"]), fm(inp["c"][b])], axis=-1)
    m["cv"] = np.ascontiguousarray(cv.reshape(128, 64))
    m["bm"] = np.ascontiguousarray(np.concatenate([fm(inp["b_mod"][l]) for l in range(DEPTH)], axis=1))
    m["nrm"] = np.ascontiguousarray(np.concatenate(
        [fm(inp[n][l]) for l in range(DEPTH) for n in ("norm_ffn1", "norm_mix", "norm_ffn2")], axis=1))
    for n in ("w_mod", "w_ffn1_gu", "w_ffn2_gu", "w_ffn1_down", "w_ffn2_down", "w_in", "w_out"):
        for l in range(DEPTH):
            m[f"{n}_{l}"] = inp[n][l]
    m["ident"] = np.eye(128, dtype=np.float32)
    perm = np.array([d * 32 + t * 16 + h for t in range(2) for d in range(2) for h in range(16)])
    gpar = np.zeros((64, 2 * DEPTH), np.float32)
    for l in range(DEPTH):
        m[f"w_ba_{l}"] = np.ascontiguousarray(inp["w_in"][l][:, GBA:GBA + 64][:, perm])
        gpar[32:64, 2 * l] = np.asarray(inp["a_log"][l], np.float32).reshape(32)
        gpar[32:64, 2 * l + 1] = np.asarray(inp["dt_bias"][l], np.float32).reshape(32)
        m[f"pool_w_{l}"] = np.ascontiguousarray(np.asarray(inp["pool_w"][l], np.float32).reshape(1024, 256))
    m["gpar"] = gpar
    m["cq"] = np.ascontiguousarray(np.concatenate(
        [fm(inp["conv_qkv"][l][j]) for l in range(DEPTH) for j in range(5)], axis=1))
    m["gnw"] = np.ascontiguousarray(np.stack([np.asarray(inp["gdn_norm"][l], np.float32) for l in range(DEPTH)], axis=1))
    m["csh"] = np.ascontiguousarray(np.concatenate(
        [fm(inp["conv_short"][l][j]) for l in range(DEPTH) for j in range(3)], axis=1))
    m["psc"] = np.ascontiguousarray(np.concatenate([fm(inp["pool_scale"][l]) for l in range(DEPTH)], axis=1))
    ii = np.arange(64)
    U = (ii[:, None] <= ii[None, :]).astype(np.float32)
    Us = (ii[:, None] < ii[None, :]).astype(np.float32)
    m["msk"] = np.ascontiguousarray(np.concatenate([U, U.T, Us, Us.T], axis=1))
    m["pcnt"] = pool_counts()
    m["nfin"] = np.ascontiguousarray(np.broadcast_to(np.asarray(inp["norm_final"], np.float32)[None, :], (128, D)))
    return m


def pool_counts():
    out = np.zeros((4, 128, NTOK), np.float32)
    for wi, w in enumerate((2, 4, 8, 16)):
        for (n, rep, base) in ((NCTX, 1, 0), (NXT // 64, 64, NCTX)):
            t = np.arange(n)
            lo = np.clip(t - w // 2, 0, n)
            hi = np.clip(t - w // 2 + w, 0, n)
            inv = np.repeat(np.float32(1.0) / (hi - lo).astype(np.float32), rep)
            out[wi, :, base:base + n * rep] = inv[None, :]
    return np.ascontiguousarray(out.reshape(4 * 128, NTOK))


_CACHE = {}


def kernel(**inputs):
    inp = {k: np.asarray(v) for k, v in inputs.items()}
    n_cores = 4
    if "nc" not in _CACHE:
        _CACHE["nc"] = build_program({})
        _CACHE["used"] = _CACHE["nc"].used_inputs
    nc = _CACHE["nc"]
    in_maps = [host_inputs(inp, b) for b in range(n_cores)]
    used = set(_CACHE["used"])
    in_maps = [{k: v for k, v in m.items() if k in used} for m in in_maps]
    res = run_bass_kernel_spmd(nc, in_maps, core_ids=list(range(n_cores)))
    out = np.stack([res.results[b]["out"] for b in range(n_cores)], axis=0)
    return out.astype(np.float32)


GQ, GK, GV, GZ, GBA, GSX, GSB, GSC, GPL = 0, 1024, 2048, 4096, 6144, 6208, 7232, 8256, 9280
SEQS = [(0, NCTX), (NCTX, NXT)]


def mixer_setup(K):
    nc = K.nc
    if K.cfg.get("pT_input"):
        K.pT = dram_in(nc, "pT", [D_IN, NTOK])
    else:
        K.pT = nc.dram_tensor("pT", [D_IN, NTOK], F32).ap()
    K.yT = nc.dram_tensor("yT", [D, NTOK], BF16).ap()
    K.pT_res = [[Res(f"pT{c}_{b}") for b in range(len(BLOCKS))] for c in range(81)]
    K.yT_res = [Res(f"yT{c}") for c in range(32)]
    K.w_ba = [dram_in(nc, f"w_ba_{l}", [D, 64]) for l in range(DEPTH)]
    K.gpar = dram_in(nc, "gpar", [64, 2 * DEPTH])
    K.cq = dram_in(nc, "cq", [128, DEPTH * 5 * 32])
    K.gnw = dram_in(nc, "gnw", [128, DEPTH])
    K.csh = dram_in(nc, "csh", [128, DEPTH * 3 * 8])
    K.psc = dram_in(nc, "psc", [128, DEPTH * 8])
    K.msk = dram_in(nc, "msk", [64, 4 * 64])
    K.pcnt = dram_in(nc, "pcnt", [4 * 128, NTOK])
    K.extra_inputs = [f"w_ba_{l}" for l in range(DEPTH)] + ["gpar", "cq", "gnw", "csh", "psc", "msk", "pcnt"]


def phase_inproj(K, l):
    nc, S = K.nc, K.S
    from contextlib import ExitStack
    w_in = W(K, "w_in", l)
    with ExitStack() as st:
        regA = st.enter_context(sbt(nc, "regA", [128, 32 * 512], BF16))
        regB = st.enter_context(sbt(nc, "regB", [128, 64 * 512], BF16))
        L = Ctx()
        L.regA, L.regB = regA, regB
        L.resA = [Res(f"A{c}") for c in range(32)]
        L.resB = [Res(f"B{c}") for c in range(64)]
        L.ss = TT(st.enter_context(sbt(nc, "ss", [128, 4], F32)), "ss")
        L.diag = [TT(st.enter_context(sbt(nc, f"diag{i}", [128, 128], F32)), f"diag{i}") for i in range(2)]
        L.pt = [PT(st.enter_context(pst(nc, f"pt{i}", [128, 512], F32)), f"pt{i}") for i in range(2)]
        wp = [TT(st.enter_context(sbt(nc, f"wp{i}", [128, 32 * 512], BF16)), f"wp{i}") for i in range(2)]
        stg = [TT(st.enter_context(sbt(nc, f"stg{i}", [128, 512], F32)), f"stg{i}") for i in range(4)]
        pp = [PT(st.enter_context(pst(nc, f"pp{i}", [128, 512], F32)), f"pp{i}") for i in range(4)]
        xnT = regA[:, :].rearrange("p (c t) -> p c t", c=32)
        wi = 0
        oi = 0
        for bi, (t0, T) in enumerate(BLOCKS):
            r = 0 if bi == 0 else 1
            front_end(K, L, K.xres, False, t0, T, l, 1, r)
            for wt in range(21):
                w = wp[wi % 2]
                wi += 1
                wv = w.t[:, :].rearrange("p (k n) -> p k n", k=32)
                if wt < 20:
                    c0 = wt * 512 if wt < 12 else wt * 512 + 64
                    wload(K, wv[:, :, :], w.res, w_in[:, c0:c0 + 512].rearrange("(kc p) n -> p kc n", p=128), 4)
                    chunks = [(q * 128, 128, (c0 - (0 if wt < 12 else 64)) // 128 + q, c0 + q * 128) for q in range(4)]
                else:
                    wload(K, wv[:, :, 0:64], w.res, K.w_ba[l].rearrange("(kc p) n -> p kc n", p=128), 1)
                    chunks = [(0, 64, 80, GBA)]
                for (off, m, ci, row0) in chunks:
                    p_ = pp[oi % 4]
                    s_ = stg[oi % 4]
                    for kc in range(32):
                        S.mm(p_[0:m, :T], V(wv[:, kc, off:off + m], w.res), V(xnT[:, kc, :T], L.resA[kc]),
                             start=(kc == 0), stop=(kc == 31))
                    if oi % 2 == 0:
                        S.copy("act", s_[0:m, :T], p_[0:m, :T])
                    else:
                        S.copy("dve", s_[0:m, :T], p_[0:m, :T])
                    S.dma("sp", V(K.pT[row0:row0 + m, t0:t0 + T], K.pT_res[ci][bi]), s_[0:m, :T])
                    oi += 1


def run_gens(gens, max_rounds=10 ** 9):
    gens = list(gens)
    rounds = 0
    while gens and rounds < max_rounds:
        rounds += 1
        nxt = []
        for g in gens:
            try:
                next(g)
                nxt.append(g)
            except StopIteration:
                pass
        gens = nxt


def chain_order(d):
    if d == 0:
        return list(range(36))
    return [3, 2, 1, 0] + list(range(35, 3, -1))


def phase_gdn(K, l):
    nc, S = K.nc, K.S
    from contextlib import ExitStack
    NB = 4
    K._gb = {}
    with ExitStack() as st:
        def sb(name, shape, dt=F32):
            return TT(st.enter_context(sbt(nc, name, shape, dt)), name)
        ps_tiles = [PT(st.enter_context(pst(nc, f"gps{i}", [128, 512], F32)), f"gps{i}") for i in range(8)]
        ps_i = [0]

        def ps():
            t = ps_tiles[ps_i[0] % 8]
            ps_i[0] += 1
            return t
        msk = sb("mskt", [64, 4, 64])
        S.dma("sp", V(msk.t[:].rearrange("p a b -> p (a b)"), msk.res), V(K.msk, K.ext))
        U, Lo, Us, Los = (msk[:, i, :] for i in range(4))
        cq = sb("cqt", [128, 5, 32])
        S.dma("sp", V(cq.t[:].rearrange("p a b -> p (a b)"), cq.res),
              V(K.cq[:, l * 160:(l + 1) * 160], K.ext))
        gpar = sb("gpart", [64, 2])
        S.dma("sp", gpar[:], V(K.gpar[:, 2 * l:2 * l + 2], K.ext))
        gnw_all = sb("gnwt", [128, DEPTH])
        S.dma("sp", gnw_all[:], V(K.gnw, K.ext))
        gnw = gnw_all[:, l:l + 1]
        baT = sb("baT", [64, NTOK])
        S.dma("sp", baT[:], V(K.pT[GBA:GBA + 64, :], [K.pT_res[80][b] for b in range(len(BLOCKS))]))
        bg = sb("bg", [64, 36, 64])
        negA = sb("negA", [64, 1])
        S.act(negA[32:64, :], gpar[32:64, 0:1], AF.Exp)
        S.ts("dve", negA[32:64, :], negA[32:64, :], -1.0, None, ALU.mult)
        S.act(baT[0:32, :], baT[0:32, :], AF.Sigmoid)
        S.ts("dve", baT[32:64, :], baT[32:64, :], gpar[32:64, 1:2], None, ALU.add)
        S.act(baT[32:64, :], baT[32:64, :], AF.Exp)
        S.ts("dve", baT[32:64, :], baT[32:64, :], 1.0, None, ALU.add)
        S.act(baT[32:64, :], baT[32:64, :], AF.Ln)
        S.ts("dve", baT[32:64, :], baT[32:64, :], negA[32:64, 0:1], None, ALU.mult)
        for c in range(36):
            p_ = ps()
            S.transpose(p_[0:64, 0:64], baT[:, c * 64:(c + 1) * 64], K.ident[0:64, 0:64])
            S.copy("dve" if c % 2 else "act", bg[:, c, :], p_[0:64, 0:64])
        if K.cfg.get("gdn_stop", 9) <= 1:
            return
        qT = sb("qT", [128, NTOK])
        kT = sb("kT", [128, NTOK])
        vT = sb("vT", [128, 2, NTOK])
        raw = [sb(f"raw{i}", [128, NTOK + 8]) for i in range(1)]
        sq = sb("sq", [128, NTOK])
        oacc = sb("oacc", [64, 36, 2, 128])
        Sst = sb("Sst", [128, 2, 128])
        ri = [0]

        def load_conv(dst, row0, chan_chunk):
            rw = raw[0]
            ri[0] += 1
            S.memset("pool", rw[:], 0.0)
            ci = row0 // 128
            for bi, (t0, T) in enumerate(BLOCKS):
                off = 2 if bi == 0 else 6
                S.dma("sp", rw[:, off + t0:off + t0 + T], V(K.pT[row0:row0 + 128, t0:t0 + T], K.pT_res[ci][bi]))
            for (s0, n) in SEQS:
                off = 2 if s0 == 0 else 6
                o = dst[:, s0:s0 + n] if isinstance(dst, TT) else V(dst.ap[:, s0:s0 + n], dst.res)
                e = "dve"
                for j in range(5):
                    src = rw[:, off + s0 + j - 2: off + s0 + j - 2 + n]
                    wj = cq[:, j, chan_chunk:chan_chunk + 1]
                    if j == 0:
                        S.ts(e, o, src, wj, None, ALU.mult)
                    else:
                        S.stt(e, o, src, wj, o, ALU.mult, ALU.add)
            full = dst[:, :] if isinstance(dst, TT) else dst
            S.act(full, full, AF.Silu)

        def l2n(t, scale):
            S.act(sq[:], t[:], AF.Square)
            for c0 in range(0, NTOK, 512):
                n = min(512, NTOK - c0)
                p_ = ps()
                S.mm(p_[:, :n], K.ones[:, :], sq[:, c0:c0 + n])
                S.ts("dve", sq[:, c0:c0 + n], p_[:, :n], EPS, None, ALU.add)
                S.act(sq[:, c0:c0 + n], sq[:, c0:c0 + n], AF.Ln)
                S.act(sq[:, c0:c0 + n], sq[:, c0:c0 + n], AF.Exp, scale=-0.5)
                S.stt("dve", t[:, c0:c0 + n], t[:, c0:c0 + n], scale, sq[:, c0:c0 + n], ALU.mult, ALU.mult)

        for g in range(K.cfg.get("n_groups", 8)):
            load_conv(qT, GQ + g * 128, g)
            load_conv(kT, GK + g * 128, 8 + g)
            for hh in range(2):
                load_conv(V(vT.t[:, hh, :], vT.res), GV + (2 * g + hh) * 128, 16 + 2 * g + hh)
            l2n(qT, 128 ** -0.5)
            l2n(kT, 1.0)
            if K.cfg.get("gdn_stop", 9) <= 2:
                continue
            for d in range(2):
                Minc, MsT = (U, Los) if d == 0 else (Lo, Us)
                mS_ij, mI_ji = (Los, U) if d == 0 else (Us, Lo)
                order = chain_order(d)
                bufs = [dict() for _ in range(NB)]
                for b in range(NB):
                    B = bufs[b]
                    for nm, shp in (("gInc", [64, 2, 64]), ("gMs", [64, 2, 64]), ("Dm", [64, 2, 64]), ("DT", [64, 2, 64]),
                                    ("KKm", [64, 64]), ("QKm", [64, 64]), ("P0", [64, 2, 64]), ("PT0", [64, 2, 64]),
                                    ("P1", [64, 2, 64]), ("PT1", [64, 2, 64]), ("X0", [64, 2, 64]), ("X1", [64, 2, 64]),
                                    ("kbg", [64, 2, 128]), ("kdec", [64, 2, 128]), ("vb", [64, 2, 128]),
                                    ("u", [64, 2, 128]), ("wTn", [128, 2, 64]), ("qdT", [128, 2, 64]),
                                    ("attnT", [64, 2, 64]), ("egl", [128, 2]), ("sc", [64, 8]), ("ktok", [64, 128])):
                        key = f"g_{b}_{nm}"
                        if key not in K._gb:
                            K._gb[key] = sb(key, shp)
                        B[nm] = K._gb[key]

                def pre(c, B, d=d, Minc=Minc, MsT=MsT, mS_ij=mS_ij, mI_ji=mI_ji):
                    tok = slice(c * 64, (c + 1) * 64)
                    beta = bg[:, c, d * 16 + 2 * g: d * 16 + 2 * g + 2]
                    gg = bg[:, c, 32 + d * 16 + 2 * g: 32 + d * 16 + 2 * g + 2]
                    sc = B["sc"]
                    for ln in range(2):
                        S.ts("pool", B["gInc"][:, ln, :], Minc, V(gg.ap[:, ln:ln + 1], gg.res), None, ALU.mult)
                        S.ts("pool", B["gMs"][:, ln, :], MsT, V(gg.ap[:, ln:ln + 1], gg.res), None, ALU.mult)
                    pd = ps()
                    for ln in range(2):
                        S.mm(pd[0:64, ln * 64:(ln + 1) * 64], Minc, B["gMs"][:, ln, :], sig=False)
                        S.mm(pd[0:64, 128 + ln * 64:128 + (ln + 1) * 64], MsT, B["gInc"][:, ln, :], sig=False)
                    S.mm(pd[0:64, 256:258], Minc, gg, sig=False)
                    S.mm(pd[:, 260:262], K.ones[0:64, :], gg)
                    pr = ps()
                    for ln in range(2):
                        S.mm(pr[:, ln * 64:(ln + 1) * 64], K.ones[0:64, :], B["gInc"][:, ln, :], sig=(ln == 1))
                    yield
                    S.act(V(B["Dm"].t[:].rearrange("p a b -> p (a b)"), B["Dm"].res), pd[0:64, 0:128], AF.Exp)
                    S.act(V(B["DT"].t[:].rearrange("p a b -> p (a b)"), B["DT"].res), pd[0:64, 128:256], AF.Exp)
                    S.act(V(B["qdT"].t[:].rearrange("p a b -> p (a b)"), B["qdT"].res), pr[:, 0:128], AF.Exp)
                    S.act(B["egl"][:], pd[:, 260:262], AF.Exp)
                    S.act(sc[:, 0:2], pd[0:64, 256:258], AF.Exp)
                    S.ts("dve", sc[:, 2:4], pd[0:64, 256:258], -1.0, None, ALU.mult)
                    S.tt("dve", sc[:, 2:4], sc[:, 2:4], pd[0:64, 260:262], ALU.add)
                    S.act(sc[:, 2:4], sc[:, 2:4], AF.Exp)
                    S.tt("dve", sc[:, 4:6], sc[:, 0:2], beta, ALU.mult)
                    S.ts("dve", sc[:, 6:8], beta, -1.0, None, ALU.mult)
                    for ln in range(2):
                        S.tt("pool", B["qdT"][:, ln, :], B["qdT"][:, ln, :], qT[:, tok], ALU.mult)
                    pg_ = ps()
                    S.mm(pg_[0:64, 0:64], kT[:, tok], kT[:, tok], sig=False)
                    S.mm(pg_[0:64, 64:128], kT[:, tok], qT[:, tok], sig=False)
                    S.transpose(pg_[0:64, 128:256], kT[:, tok], K.ident[:, :], sig=False)
                    S.transpose(pg_[0:64, 256:384], vT[:, 0, tok], K.ident[:, :], sig=False)
                    S.transpose(pg_[0:64, 384:512], vT[:, 1, tok], K.ident[:, :])
                    yield
                    S.tt("dve", B["KKm"][:], pg_[0:64, 0:64], mS_ij, ALU.mult)
                    S.tt("dve", B["QKm"][:], pg_[0:64, 64:128], mI_ji, ALU.mult)
                    S.copy("act", B["ktok"][:], pg_[0:64, 128:256])
                    for ln in range(2):
                        S.ts("dve", B["vb"][:, ln, :], pg_[0:64, 256 + ln * 128:384 + ln * 128], V(beta.ap[:, ln:ln + 1], beta.res), None, ALU.mult)
                        S.ts("pool", B["kbg"][:, ln, :], B["ktok"][:], sc[:, 4 + ln:5 + ln], None, ALU.mult)
                        S.ts("pool", B["kdec"][:, ln, :], B["ktok"][:], sc[:, 2 + ln:3 + ln], None, ALU.mult)
                        S.stt("dve", B["P0"][:, ln, :], B["KKm"][:], sc[:, 6 + ln:7 + ln], B["Dm"][:, ln, :], ALU.mult, ALU.mult)
                        S.tt("pool", B["attnT"][:, ln, :], B["QKm"][:], B["DT"][:, ln, :], ALU.mult)
                    yield
                    pn = ps()
                    for ln in range(2):
                        S.transpose(pn[0:64, ln * 64:(ln + 1) * 64], B["P0"][:, ln, :], K.ident[0:64, 0:64], sig=(ln == 1))
                    yield
                    var = K.cfg.get("r5", 3)
                    S.copy("dve" if var == 1 else "act", V(B["PT0"].t[:].rearrange("p a b -> p (a b)"), B["PT0"].res), pn[0:64, 0:128])
                    for ln in range(2):
                        if var == 2:
                            pass
                        elif var == 3:
                            S.tt("dve", B["X0"][:, ln, :], B["PT0"][:, ln, :], K.ident[0:64, 0:64], ALU.add)
                        else:
                            S.tt("dve", B["X0"][:, ln, :], pn[0:64, ln * 64:(ln + 1) * 64], K.ident[0:64, 0:64], ALU.add)
                    yield
                    cur, nxt = ("P0", "PT0", "X0"), ("P1", "PT1", "X1")
                    for lvl in range(5):
                        Pc, PTc, Xc = (B[n] for n in cur)
                        Pn, PTn, Xn = (B[n] for n in nxt)
                        p2 = ps()
                        for ln in range(2):
                            S.mm(p2[0:64, ln * 64:(ln + 1) * 64], PTc[:, ln, :], Pc[:, ln, :], sig=False)
                            S.mm(p2[0:64, 128 + ln * 64:128 + (ln + 1) * 64], Pc[:, ln, :], PTc[:, ln, :], sig=(ln == 1))
                        yield
                        S.copy("act", V(Pn.t[:].rearrange("p a b -> p (a b)"), Pn.res), p2[0:64, 0:128])
                        if lvl < 4:
                            S.copy("dve", V(PTn.t[:].rearrange("p a b -> p (a b)"), PTn.res), p2[0:64, 128:256])
                        px = ps()
                        for ln in range(2):
                            S.mm(px[0:64, ln * 64:(ln + 1) * 64], K.ident[0:64, 0:64], Xc[:, ln, :], start=True, stop=False, sig=False)
                            S.mm(px[0:64, ln * 64:(ln + 1) * 64], Pn[:, ln, :], Xc[:, ln, :], start=False, stop=True, sig=(ln == 1))
                        yield
                        S.copy("dve", V(Xn.t[:].rearrange("p a b -> p (a b)"), Xn.res), px[0:64, 0:128])
                        cur, nxt = nxt, cur
                    Xf = B[cur[2]]
                    pu = ps()
                    for ln in range(2):
                        S.mm(pu[0:64, ln * 128:(ln + 1) * 128], Xf[:, ln, :], B["vb"][:, ln, :], sig=False)
                        S.mm(pu[:, 256 + ln * 64:256 + (ln + 1) * 64], B["kbg"][:, ln, :], Xf[:, ln, :], sig=(ln == 1))
                    yield
                    S.copy("act", V(B["u"].t[:].rearrange("p a b -> p (a b)"), B["u"].res), pu[0:64, 0:256])
                    S.ts("dve", V(B["wTn"].t[:].rearrange("p a b -> p (a b)"), B["wTn"].res), pu[:, 256:384], -1.0, None, ALU.mult)

                if "vnew0" not in K._gb:
                    for i in range(2):
                        K._gb[f"vnew{i}"] = sb(f"vnew{i}", [64, 2, 128])
                vnew = [K._gb["vnew0"], K._gb["vnew1"]]

                def steps(chunks, bset, si0, d=d):
                    for i, c in enumerate(chunks):
                        B = bset[i]
                        vn = vnew[(si0 + i) % 2]
                        pv = ps()
                        for ln in range(2):
                            S.mm(pv[0:64, ln * 128:(ln + 1) * 128], B["wTn"][:, ln, :], Sst[:, ln, :], sig=(ln == 1))
                        yield
                        S.tt("dve", V(vn.t[:].rearrange("p a b -> p (a b)"), vn.res), pv[0:64, 0:256],
                             V(B["u"].t[:].rearrange("p a b -> p (a b)"), B["u"].res), ALU.add)
                        po = ps()
                        for ln in range(2):
                            S.mm(po[0:64, ln * 128:(ln + 1) * 128], B["qdT"][:, ln, :], Sst[:, ln, :],
                                 start=True, stop=False, sig=False)
                            S.mm(po[0:64, ln * 128:(ln + 1) * 128], B["attnT"][:, ln, :], vn[:, ln, :],
                                 start=False, stop=True, sig=False)
                            S.mm(po[:, 256 + ln * 128:256 + (ln + 1) * 128], B["kdec"][:, ln, :], vn[:, ln, :],
                                 sig=(ln == 1))
                        yield
                        oc = V(oacc.t[:, c, :, :].rearrange("p a b -> p (a b)"), oacc.res)
                        if d == 0:
                            S.copy("act", oc, po[0:64, 0:256])
                        else:
                            S.tt("dve", oc, po[0:64, 0:256], oc, ALU.add)
                        for ln in range(2):
                            S.stt("dve", Sst[:, ln, :], Sst[:, ln, :], B["egl"][:, ln:ln + 1],
                                  po[:, 256 + ln * 128:256 + (ln + 1) * 128], ALU.mult, ALU.add)

                S.memset("dve", V(Sst.t[:].rearrange("p a b -> p (a b)"), Sst.res), 0.0)
                nsteps = K.cfg.get("n_steps", 36)
                order = order[:nsteps]
                batches = [order[i:i + 2] for i in range(0, len(order), 2)]
                run_gens([pre(c, bufs[i]) for i, c in enumerate(batches[0])], K.cfg.get("pre_rounds", 10 ** 9))
                if K.cfg.get("gdn_stop", 9) <= 3:
                    continue
                for bi_, batch in enumerate(batches):
                    cur = bufs[0:2] if bi_ % 2 == 0 else bufs[2:4]
                    oth = bufs[2:4] if bi_ % 2 == 0 else bufs[0:2]
                    gens = [steps(batch, cur, bi_ * 2)]
                    if bi_ + 1 < len(batches):
                        gens += [pre(c, oth[i]) for i, c in enumerate(batches[bi_ + 1])]
                    run_gens(gens)
            if K.cfg.get("gdn_stop", 9) >= 5:
                gdn_out(K, l, g, oacc, gnw, ps, sb)


def gdn_out(K, l, g, oacc, gnw, ps, sb):
    S = K.S
    if "go_z" not in K._gb:
        K._gb["go_z"] = [sb(f"go_z{i}", [128, 2, 512]) for i in range(2)]
        K._gb["go_y"] = [sb(f"go_y{i}", [128, 2, 512], BF16) for i in range(2)]
        K._gb["go_s"] = sb("go_s", [64, 36, 2, 2])
        K._gb["go_j"] = sb("go_j", [64, 128])
        K._gb["go_o"] = [sb(f"go_o{i}", [64, 2, 128]) for i in range(2)]
    ss = K._gb["go_s"]
    for c in range(36):
        for ln in range(2):
            S.act(K._gb["go_j"][:], oacc[:, c, ln, :], AF.Square, accum=ss[:, c, ln, 0:1])
    ssf = V(ss.t[:, :, :, 0], ss.res)
    rsf = V(ss.t[:, :, :, 1], ss.res)
    S.act(rsf, ssf, AF.Ln, bias=K.consts[0:64, 0:1], scale=1.0 / 128)
    S.act(rsf, rsf, AF.Exp, scale=-0.5)
    for bi, (t0, T) in enumerate(BLOCKS):
        z = K._gb["go_z"][bi % 2]
        y = K._gb["go_y"][bi % 2]
        for ln in range(2):
            h = 2 * g + ln
            S.dma("sp", z[:, ln, :T], V(K.pT[GZ + h * 128:GZ + (h + 1) * 128, t0:t0 + T], K.pT_res[(GZ // 128) + h][bi]))
        S.act(V(z.t[:, :, :T], z.res), V(z.t[:, :, :T], z.res), AF.Silu)
        for ci in range(T // 64):
            c = t0 // 64 + ci
            o_ = K._gb["go_o"][ci % 2]
            p_ = ps()
            for ln in range(2):
                S.ts("pool", o_[:, ln, :], oacc[:, c, ln, :], ss[:, c, ln, 1:2], None, ALU.mult)
            for ln in range(2):
                S.transpose(p_[:, ln * 64:(ln + 1) * 64], o_[:, ln, :], K.ident[0:64, 0:64], sig=(ln == 1))
            for ln in range(2):
                S.stt("dve", y[:, ln, ci * 64:(ci + 1) * 64], p_[:, ln * 64:(ln + 1) * 64], gnw,
                      z[:, ln, ci * 64:(ci + 1) * 64], ALU.mult, ALU.mult)
        for ln in range(2):
            h = 2 * g + ln
            S.dma("act", V(K.yT[h * 128:(h + 1) * 128, t0:t0 + T], K.yT_res[h]), y[:, ln, :T])


def pT_res_rows(K, row0):
    ci = row0 // 128 if row0 < GBA else (row0 - 64) // 128
    return K.pT_res[ci]


def phase_sc_pool(K, l):
    nc, S = K.nc, K.S
    from contextlib import ExitStack
    with ExitStack() as st:
        def sb(name, shape, dt=F32):
            return TT(st.enter_context(sbt(nc, name, shape, dt)), name)
        csh = sb("csht", [128, 3, 8])
        S.dma("sp", V(csh.t[:].rearrange("p a b -> p (a b)"), csh.res), V(K.csh[:, l * 24:(l + 1) * 24], K.ext))
        psc = sb("psct", [128, 8])
        S.dma("sp", psc[:], V(K.psc[:, l * 8:(l + 1) * 8], K.ext))
        xin = [sb(f"scx{i}", [128, NTOK]) for i in range(2)]
        bgt = [sb(f"scb{i}", [128, NTOK]) for i in range(2)]
        cgt = [sb(f"scc{i}", [128, NTOK]) for i in range(2)]
        acc = [sb(f"sca{i}", [128, NTOK]) for i in range(2)]
        yb = [sb(f"scy{i}", [128, NTOK], BF16) for i in range(2)]

        def load(dst, row0):
            rr = pT_res_rows(K, row0)
            for bi, (t0, T) in enumerate(BLOCKS):
                S.dma("sp", dst[:, t0:t0 + T], V(K.pT[row0:row0 + 128, t0:t0 + T], rr[bi]))

        for cc in range(8):
            x_, b_, c_, a_, y_ = xin[cc % 2], bgt[cc % 2], cgt[cc % 2], acc[cc % 2], yb[cc % 2]
            load(x_, GSX + cc * 128)
            load(b_, GSB + cc * 128)
            load(c_, GSC + cc * 128)
            S.tt("pool", x_[:], x_[:], c_[:], ALU.mult)
            S.ts("dve", a_[:], x_[:], csh[:, 1, cc:cc + 1], None, ALU.mult)
            for (s0, n, wdt) in ((0, NCTX, NCTX), (NCTX, NXT, 64)):
                v3 = x_.t[:, s0:s0 + n].rearrange("p (r w) -> p r w", w=wdt)
                a3 = a_.t[:, s0:s0 + n].rearrange("p (r w) -> p r w", w=wdt)
                S.stt("dve", V(a3[:, :, 1:], a_.res), V(v3[:, :, :wdt - 1], x_.res), csh[:, 0, cc:cc + 1],
                      V(a3[:, :, 1:], a_.res), ALU.mult, ALU.add)
                S.stt("dve", V(a3[:, :, :wdt - 1], a_.res), V(v3[:, :, 1:], x_.res), csh[:, 2, cc:cc + 1],
                      V(a3[:, :, :wdt - 1], a_.res), ALU.mult, ALU.add)
            S.tt("pool", y_[:], a_[:], b_[:], ALU.mult)
            S.dma("act", V(K.yT[2048 + cc * 128:2048 + (cc + 1) * 128, :], K.yT_res[16 + cc]), y_[:])
        pw = W(K, "pool_w", l)
        pwt = sb("pwt", [128, 2, 256], BF16)
        cnt = sb("pcntt", [128, NTOK])
        dfb = [sb(f"pdf{i}", [128, NTOK], BF16) for i in range(2)]
        PADX = NTOK + 16 * 64 + 16
        bufA = sb("plA", [128, PADX])
        bufB = sb("plB", [128, PADX])
        pps = [PT(st.enter_context(pst(nc, f"plp{i}", [128, 512], F32)), f"plp{i}") for i in range(2)]
        XO = NCTX + 16
        for gi_ in range(4):
            win = (2, 4, 8, 16)[gi_]
            nlev = gi_ + 1
            S.dma("sp", cnt[:], V(K.pcnt[gi_ * 128:(gi_ + 1) * 128, :], K.ext))
            S.dma("pool", V(pwt.t[:, :, :], pwt.res),
                  V(pw[gi_ * 256:(gi_ + 1) * 256, :].rearrange("(c p) n -> p c n", p=128), K.ext))
            for c2 in range(2):
                pc = gi_ * 2 + c2
                u_ = xin[c2]
                load(u_, GPL + pc * 128)
                S.memset("pool", bufA[:], 0.0)
                S.copy("pool", bufA[:, 8:8 + NCTX], u_[:, 0:NCTX])
                S.copy("pool", bufA[:, XO + 8 * 64:XO + 40 * 64], u_[:, NCTX:NTOK])
                src, dst = bufA, bufB
                for lev in range(nlev):
                    sh = 1 if lev == 0 else 2 ** (lev - 1)
                    lo = (1, 2, 4, 8)[lev]
                    for (base, n, wdt) in ((0, NCTX, 1), (XO, NXT // 64, 64)):
                        hi = n + 16 - lo + (1 if lev == 0 else 0)
                        hi = min(hi, n + 16 - (0 if lev == 0 else sh))
                        a0 = src.t[:, base + (lo - sh) * wdt: base + (hi - sh) * wdt]
                        if lev == 0:
                            a1 = src.t[:, base + lo * wdt: base + hi * wdt]
                        else:
                            a1 = src.t[:, base + (lo + sh) * wdt: base + (hi + sh) * wdt]
                        S.tt("dve", V(dst.t[:, base + lo * wdt: base + hi * wdt], dst.res), V(a0, src.res),
                             V(a1, src.res), ALU.add)
                    src, dst = dst, src
                S.tt("dve", a := acc[c2][:, 0:NCTX], V(src.t[:, 8:8 + NCTX], src.res), cnt[:, 0:NCTX], ALU.mult)
                S.tt("dve", acc[c2][:, NCTX:NTOK], V(src.t[:, XO + 8 * 64:XO + 40 * 64], src.res), cnt[:, NCTX:NTOK],
                     ALU.mult)
                S.tt("pool", dfb[c2][:], acc[c2][:], u_[:], ALU.subtract)
            for dd in range(2):
                y_ = yb[dd]
                for bi, (t0, T) in enumerate(BLOCKS):
                    p_ = pps[bi % 2]
                    for c2 in range(2):
                        S.mm(p_[:, :T], pwt[:, c2, dd * 128:(dd + 1) * 128], dfb[c2][:, t0:t0 + T],
                             start=(c2 == 0), stop=(c2 == 1))
                    S.ts("dve", y_[:, t0:t0 + T], p_[:, :T], psc[:, gi_ * 2 + dd:gi_ * 2 + dd + 1], None, ALU.mult)
                row = 3072 + (gi_ * 2 + dd) * 128
                S.dma("act", V(K.yT[row:row + 128, :], K.yT_res[24 + gi_ * 2 + dd]), y_[:])


def phase_outproj(K, l):
    nc, S = K.nc, K.S
    from contextlib import ExitStack
    last = (l == DEPTH - 1)
    w_out = W(K, "w_out", l)
    with ExitStack() as st:
        regA = st.enter_context(sbt(nc, "regA", [128, 32 * 512], BF16))
        yTs = st.enter_context(sbt(nc, "yTs", [128, 32 * 512], BF16))
        L = Ctx()
        L.regA = regA
        L.resA = [Res(f"A{c}") for c in range(32)]
        resY = [Res(f"Y{c}") for c in range(32)]
        L.wp = [TT(st.enter_context(sbt(nc, f"wp{i}", [128, 32 * 512], BF16)), f"wp{i}") for i in range(2)]
        L.slab = [TT(st.enter_context(sbt(nc, f"slab{i}", [128, 256], F32)), f"slab{i}") for i in range(3)]
        L.oslab = [TT(st.enter_context(sbt(nc, f"oslab{i}", [128, 256], F32)), f"oslab{i}") for i in range(3)]
        L.py = [PT(st.enter_context(pst(nc, f"py{i}", [128, 512], F32)), f"py{i}") for i in range(2)]
        L.wi, L.yi, L.si = 0, 0, 0
        y3 = yTs[:, :].rearrange("p (c t) -> p c t", c=32)
        for bi, (t0, T) in enumerate(BLOCKS):
            if bi == 0 and last:
                continue
            r = 0 if bi == 0 else 1
            for c in range(32):
                S.dma("sp", V(y3[:, c, :T], resY[c]), V(K.yT[c * 128:(c + 1) * 128, t0:t0 + T], K.yT_res[c]))
            down_stage(K, L, l, 1, r, t0, T, y3, resY, 32, w_out, False)


def phase_mixer(K, l):
    S = K.S
    parts = K.cfg.get("mix_parts", "igso")
    if "i" in parts:
        phase_inproj(K, l)
        S.barrier()
    if "g" in parts:
        phase_gdn(K, l)
        S.barrier()
    if "s" in parts:
        phase_sc_pool(K, l)
        S.barrier()
    if "o" in parts:
        phase_outproj(K, l)
        S.barrier()
```

```python
import numpy as np
import concourse.bass as bass
import concourse.mybir as mybir
from concourse.bass_utils import run_bass_kernel_spmd

F32 = mybir.dt.float32
BF16 = mybir.dt.bfloat16
AF = mybir.ActivationFunctionType
ALU = mybir.AluOpType
AX = mybir.AxisListType

SAME_ENGINE_SYNC = True
N_DMA_SEMS = 24


class Res:
    __slots__ = ("name", "w", "r", "excl")

    def __init__(self, name=""):
        self.name = name
        self.w = None
        self.r = {}
        self.excl = False


class V:
    __slots__ = ("ap", "res")

    def __init__(self, ap, res):
        self.ap = ap
        self.res = list(res) if isinstance(res, (list, tuple)) else [res]


class TT:
    def __init__(self, t, name="", nres=1):
        self.t = t
        self.res = [Res(f"{name}{i}") for i in range(nres)]

    def __getitem__(self, idx):
        return V(self.t[idx], self.res)

    def v(self, ap, ri=None):
        if ri is None:
            return V(ap, self.res)
        if isinstance(ri, int):
            return V(ap, [self.res[ri]])
        return V(ap, [self.res[i] for i in ri])


class PT(TT):
    def __init__(self, t, name="", nres=1):
        super().__init__(t, name, nres)
        for r in self.res:
            r.excl = True


class Sched:
    def __init__(self, nc):
        self.nc = nc
        self.engs = {"pe": nc.tensor, "act": nc.scalar, "dve": nc.vector, "pool": nc.gpsimd, "sp": nc.sync}
        self.sems = {}
        self.cnt = {}
        for e in self.engs:
            self.sems[e] = nc.alloc_semaphore(name=f"c_{e}")
            self.cnt[e] = 0
        self.seen = {e: {} for e in self.engs}
        self.pending = {e: [] for e in self.engs}
        self.dma_sems = {}
        self.dma_rr = {}
        for q in ("sp", "act", "pool"):
            keys = []
            for i in range(N_DMA_SEMS):
                k = f"d_{q}{i}"
                self.sems[k] = nc.alloc_semaphore(name=k)
                self.cnt[k] = 0
                keys.append(k)
            self.dma_sems[q] = keys
            self.dma_rr[q] = 0
        self.n_wait = 0
        self.n_ins = 0

    def _wait(self, e, key, val):
        if val <= 0:
            return
        if key == e and not SAME_ENGINE_SYNC:
            return
        if self.seen[e].get(key, 0) >= val:
            return
        self.engs[e].wait_ge(self.sems[key], val)
        self.seen[e][key] = val
        self.n_wait += 1

    def _deps(self, e, reads, writes):
        needs = {}
        for v in reads:
            for r in v.res:
                if r.w is not None:
                    if r.w[0] == "pending":
                        raise RuntimeError(f"read of unsignalled resource {r.name}")
                    needs[r.w[0]] = max(needs.get(r.w[0], 0), r.w[1])
                if r.excl:
                    for k, val in r.r.items():
                        if k != "pending" and k != e:
                            needs[k] = max(needs.get(k, 0), val)
        for v in writes:
            for r in v.res:
                if r.w is not None:
                    if r.w[0] == "pending":
                        if r.w[1] != e:
                            raise RuntimeError(f"write of unsignalled resource {r.name}")
                    else:
                        needs[r.w[0]] = max(needs.get(r.w[0], 0), r.w[1])
                for k, val in r.r.items():
                    if k == "pending":
                        if val != e:
                            raise RuntimeError(f"WAR on unsignalled resource {r.name}")
                        continue
                    needs[k] = max(needs.get(k, 0), val)
        for k, val in needs.items():
            self._wait(e, k, val)

    def _mark(self, key, val, reads, writes):
        for v in reads:
            for r in v.res:
                r.r.pop("pending", None)
                r.r[key] = max(r.r.get(key, 0), val)
        for v in writes:
            for r in v.res:
                r.w = (key, val)
                r.r = {}

    def issue(self, e, reads, writes, emit, sig=True):
        reads = [v for v in reads if isinstance(v, V)]
        writes = [v for v in writes if isinstance(v, V)]
        self._deps(e, reads, writes)
        ins = emit()
        self.n_ins += 1
        if sig:
            self.cnt[e] += 1
            ins.then_inc(self.sems[e], 1)
            val = self.cnt[e]
            for (rd, wr) in self.pending[e]:
                self._mark(e, val, rd, wr)
            self.pending[e] = []
            self._mark(e, val, reads, writes)
        else:
            self.pending[e].append((reads, writes))
            for v in reads:
                for r in v.res:
                    r.r["pending"] = e
            for v in writes:
                for r in v.res:
                    r.w = ("pending", e)
                    r.r = {}
        return ins

    def dma(self, q, out, in_, **kw):
        e = q
        self._deps(e, [in_], [out])
        keys = self.dma_sems[q]
        k = keys[self.dma_rr[q] % len(keys)]
        self.dma_rr[q] += 1
        self._wait(e, k, self.cnt[k])
        ins = self.engs[e].dma_start(out=out.ap, in_=in_.ap, **kw)
        self.cnt[k] += 16
        ins.then_inc(self.sems[k], 16)
        self.n_ins += 1
        self._mark(k, self.cnt[k], [in_], [out])
        return ins

    def barrier(self):
        for e in self.engs:
            assert not self.pending[e], f"pending on {e} at barrier"
        for e in self.engs:
            for k, val in self.cnt.items():
                if k != e:
                    self._wait(e, k, val)

    def finish(self):
        for e in self.engs:
            assert not self.pending[e]
        for k, val in self.cnt.items():
            if k != "sp":
                self._wait("sp", k, val)

    @staticmethod
    def _a(x):
        return x.ap if isinstance(x, V) else x

    def act(self, out, in_, func, bias=0.0, scale=1.0, accum=None, e="act"):
        kw = {}
        if accum is not None:
            kw["accum_out"] = accum.ap
        return self.issue(e, [in_, bias, scale], [out, accum],
                          lambda: self.engs[e].activation(out=out.ap, in_=in_.ap, func=func, bias=self._a(bias),
                                                          scale=self._a(scale), **kw))

    def tt(self, e, out, in0, in1, op):
        return self.issue(e, [in0, in1], [out],
                          lambda: self.engs[e].tensor_tensor(out=out.ap, in0=in0.ap, in1=in1.ap, op=op))

    def ts(self, e, out, in0, s1, s2, op0, op1=None, accum=None):
        kw = {}
        if op1 is not None:
            kw["op1"] = op1
        if accum is not None:
            kw["accum_out"] = accum.ap
        return self.issue(e, [in0, s1, s2], [out, accum],
                          lambda: self.engs[e].tensor_scalar(out=out.ap, in0=in0.ap, scalar1=self._a(s1),
                                                             scalar2=self._a(s2), op0=op0, **kw))

    def stt(self, e, out, in0, scalar, in1, op0, op1):
        return self.issue(e, [in0, scalar, in1], [out],
                          lambda: self.engs[e].scalar_tensor_tensor(out=out.ap, in0=in0.ap, scalar=self._a(scalar),
                                                                    in1=in1.ap, op0=op0, op1=op1))

    def copy(self, e, out, in_):
        if e == "act":
            return self.issue(e, [in_], [out], lambda: self.engs[e].copy(out=out.ap, in_=in_.ap))
        return self.issue(e, [in_], [out], lambda: self.engs[e].tensor_copy(out=out.ap, in_=in_.ap))

    def memset(self, e, out, val):
        return self.issue(e, [], [out], lambda: self.engs[e].memset(out.ap, val))

    def mm(self, out, lhsT, rhs, start=True, stop=True, sig=None):
        if sig is None:
            sig = stop
        return self.issue("pe", [lhsT, rhs], [out],
                          lambda: self.engs["pe"].matmul(out.ap, lhsT.ap, rhs.ap, start=start, stop=stop), sig=sig)

    def transpose(self, out, in_, ident, sig=True):
        return self.issue("pe", [in_, ident], [out],
                          lambda: self.engs["pe"].transpose(out.ap, in_.ap, ident.ap), sig=sig)


D = 4096
NCH = 32
DFF = 8192
NCTX = 256
NXT = 2048
NTOK = NCTX + NXT
NTILE = NTOK // 128
DEPTH = 2
NMOD = 9
EPS = 1e-6
D_IN = 10304
BLOCKS = [(0, 256), (256, 512), (768, 512), (1280, 512), (1792, 512)]


class Ctx:
    pass


_UID = [0]


def sbt(nc, name, shape, dt):
    _UID[0] += 1
    return nc.sbuf_tensor(f"{name}_{_UID[0]}", shape, dt)


def pst(nc, name, shape, dt):
    _UID[0] += 1
    return nc.psum_tensor(f"{name}_{_UID[0]}", shape, dt)


def dram_in(nc, name, shape, dt=F32):
    return nc.dram_tensor(name, list(shape), dt, kind="ExternalInput").ap()


def build_program(cfg):
    nc = bass.Bass("TRN2", target_bir_lowering=False)
    S = Sched(nc)
    K = Ctx()
    K.nc, K.S, K.cfg = nc, S, cfg
    K.xin = dram_in(nc, "xin", [NTOK, D])
    K.cv = dram_in(nc, "cv", [128, 64])
    K.wshape = {"w_mod": [D, NMOD * D], "w_ffn1_gu": [D, 2 * DFF], "w_ffn2_gu": [D, 2 * DFF],
                "w_ffn1_down": [DFF, D], "w_ffn2_down": [DFF, D], "w_in": [D, D_IN], "w_out": [D, D],
                "pool_w": [4 * 256, 256]}
    K.wts = {}
    K.bm = dram_in(nc, "bm", [128, DEPTH * 288])
    K.nrm = dram_in(nc, "nrm", [128, DEPTH * 3 * 32])
    K.identd = dram_in(nc, "ident", [128, 128])
    K.nfin = dram_in(nc, "nfin", [128, D])
    K.out = nc.dram_tensor("out", [NXT, D], F32, kind="ExternalOutput").ap()
    K.xres = nc.dram_tensor("xres", [NTOK, D], F32).ap()
    K.gvec = nc.dram_tensor("gvec", [DEPTH * 3 * 2 * 32, 128], F32).ap()
    K.xres_res = [[Res(f"xres{t}_{f}") for f in range(16)] for t in range(NTILE)]
    K.wcache = nc.dram_tensor("wcache", [48, 128, 16384], BF16).ap()
    K.wc_res = [Res(f"wc{i}") for i in range(48)]
    K.xin_res = Res("xin")
    K.out_res = Res("out")
    K.gvec_res = Res("gvec")
    K.ext = Res("ext")
    K.ident = TT(nc.alloc_sbuf_tensor("identt", [128, 128], F32), "ident")
    K.ones = TT(nc.alloc_sbuf_tensor("onest", [128, 128], F32), "ones")
    K.modT = TT(nc.alloc_sbuf_tensor("modT", [128, DEPTH, 288, 2], F32), "modT")
    K.Amod = TT(nc.alloc_sbuf_tensor("Amod", [128, DEPTH * 3 * 2, 32], F32), "Amod")
    K.nrmt = TT(nc.alloc_sbuf_tensor("nrmt", [128, DEPTH * 3, 32], F32), "nrmt")
    S.dma("sp", K.ident[:], V(K.identd, K.ext))
    S.dma("sp", V(K.nrmt.t[:].rearrange("p a b -> p (a b)"), K.nrmt.res), V(K.nrm, K.ext))
    S.memset("dve", K.ones[:], 1.0)
    K.consts = TT(nc.alloc_sbuf_tensor("constst", [128, 4], F32), "consts")
    S.memset("dve", K.consts[:, 0:1], EPS)
    S.memset("dve", K.consts[:, 1:2], 1.0)
    S.memset("dve", K.consts[:, 2:3], 0.0)
    mixer_setup(K)
    if not cfg.get("no_mod"):
        phase_mod(K)
    S.barrier()
    if cfg.get("dump_mod"):
        S.dma("sp", V(K.out[0:128, 0:DEPTH * 288 * 2], K.out_res),
              V(K.modT.t[:].rearrange("p a b c -> p (a b c)"), K.modT.res))
        S.dma("sp", V(K.out[128:256, 0:DEPTH * 3 * 2 * 32], K.out_res),
              V(K.Amod.t[:].rearrange("p a b -> p (a b)"), K.Amod.res))
        S.dma("sp", V(K.out[256:256 + DEPTH * 3 * 2 * 32, 0:128], K.out_res), V(K.gvec, K.gvec_res))
        S.finish()
        nc.used_inputs = used_inputs(K)
        return nc
    for l in range(DEPTH):
        if not cfg.get("no_ffn1"):
            phase_ffn(K, l, 0)
            S.barrier()
        if cfg.get("stop") == ("ffn1", l):
            break
        phase_mixer(K, l)
        if cfg.get("stop") == ("mix", l):
            break
        phase_ffn(K, l, 2)
        S.barrier()
    if cfg.get("no_dump"):
        pass
    elif cfg.get("dump_yT"):
        dump_yT(K)
    elif cfg.get("dump_xres"):
        dump_xres(K)
    else:
        phase_final(K)
    S.finish()
    K.stats = (S.n_ins, S.n_wait)
    nc.used_inputs = used_inputs(K)
    return nc


def used_inputs(K):
    return ["xin", "cv", "bm", "nrm", "ident", "nfin"] + list(K.wts.keys()) + list(getattr(K, "extra_inputs", []))


def phase_mod(K):
    nc, S = K.nc, K.S
    from contextlib import ExitStack
    with ExitStack() as st:
        cvt = TT(st.enter_context(sbt(nc, "cvt", [128, 64], F32)), "cvt")
        sT = TT(st.enter_context(sbt(nc, "sT", [128, 32, 2], BF16)), "sT")
        bmt = TT(st.enter_context(sbt(nc, "bmt", [128, DEPTH, 288], F32)), "bmt")
        wm = [TT(st.enter_context(sbt(nc, f"wm{i}", [128, 32, 512], BF16)), f"wm{i}") for i in range(3)]
        pm = [PT(st.enter_context(pst(nc, f"pm{i}", [128, 8], F32)), f"pm{i}") for i in range(2)]
        gT = TT(st.enter_context(sbt(nc, "gT", [128, 32], F32)), "gT")
        gR = TT(st.enter_context(sbt(nc, "gR", [32, 128], F32)), "gR")
        pg = PT(st.enter_context(pst(nc, "pgT", [32, 128], F32)), "pgT")
        S.dma("sp", cvt[:], V(K.cv, K.ext))
        S.dma("sp", V(bmt.t[:].rearrange("p a b -> p (a b)"), bmt.res), V(K.bm, K.ext))
        S.act(V(sT.t[:].rearrange("p a b -> p (a b)"), sT.res), cvt[:], AF.Silu)
        nblk = K.cfg.get("mod_blocks", 72)
        if K.cfg.get("skip_mod"):
            nblk = 0
            S.memset("dve", V(K.modT.t[:].rearrange("p a b c -> p (a b c)"), K.modT.res), 0.25)
        i = 0
        for l in range(DEPTH):
            for blk in range(nblk):
                w = wm[i % 3]
                p = pm[i % 2]
                i += 1
                src = W(K, "w_mod", l)[:, blk * 512:(blk + 1) * 512].rearrange("(kc p) n -> p kc n", p=128)
                wload(K, w.t[:, :, :], w.res, src, K.cfg.get("wsplit", 4))
                for q in range(4):
                    for kc in range(32):
                        S.mm(p[:, q * 2:(q + 1) * 2], w[:, kc, q * 128:(q + 1) * 128], sT[:, kc, :],
                             start=(kc == 0), stop=(kc == 31))
                for r in range(2):
                    S.tt("dve", K.modT[:, l, blk * 4:(blk + 1) * 4, r], p[:, r:8:2], bmt[:, l, blk * 4:(blk + 1) * 4],
                         ALU.add)
        for l in range(DEPTH if not K.cfg.get("no_derived") else 0):
            for s in range(3):
                for r in range(2):
                    idx = (l * 3 + s) * 2 + r
                    S.stt("dve", K.Amod[:, idx, :], K.modT[:, l, (3 * s + 1) * 32:(3 * s + 2) * 32, r], 1.0,
                          K.nrmt[:, l * 3 + s, :], ALU.add, ALU.mult)
                    S.ts("dve", gT[:], K.modT[:, l, (3 * s + 2) * 32:(3 * s + 3) * 32, r],
                         0.5 if s != 1 else 1.0, None, ALU.mult)
                    S.transpose(pg[:], gT[:], K.ident[:])
                    S.copy("dve", gR[:], pg[:])
                    S.dma("sp", V(K.gvec[idx * 32:(idx + 1) * 32, :], K.gvec_res), gR[:])


def W(K, name, l):
    key = f"{name}_{l}"
    if key not in K.wts:
        K.wts[key] = dram_in(K.nc, key, K.wshape[name])
    return K.wts[key]


def wload(K, dst3, res, src3, nsplit):
    k = dst3.shape[1]
    step = k // nsplit
    for i in range(nsplit):
        K.S.dma("pool", V(dst3[:, i * step:(i + 1) * step, :], res), V(src3[:, i * step:(i + 1) * step, :], K.ext))


def wtile(K, L, tile_id, first, nelem, loader):
    S = K.S
    w = L.wp[L.wi % 2]
    L.wi += 1
    if first or K.cfg.get("no_wcache"):
        loader(w)
        if not K.cfg.get("no_wcache"):
            S.dma("sp", V(K.wcache[tile_id, :, 0:nelem], K.wc_res[tile_id]), V(w.t[:, 0:nelem], w.res))
    else:
        S.dma("pool", V(w.t[:, 0:nelem], w.res), V(K.wcache[tile_id, :, 0:nelem], K.wc_res[tile_id]))
    return w


def shiftv(K, l, s, r, c):
    return K.modT[:, l, 3 * s * 32 + c, r:r + 1]


def phase_ffn(K, l, s):
    nc, S = K.nc, K.S
    from contextlib import ExitStack
    last = (l == DEPTH - 1)
    parts = K.cfg.get("ffn_parts", "fgrd")
    w_gu = W(K, "w_ffn1_gu" if s == 0 else "w_ffn2_gu", l) if "g" in parts else None
    w_dn = W(K, "w_ffn1_down" if s == 0 else "w_ffn2_down", l) if "d" in parts else None
    with ExitStack() as st:
        regA = st.enter_context(sbt(nc, "regA", [128, 32 * 512], BF16))
        regB = st.enter_context(sbt(nc, "regB", [128, 64 * 512], BF16))
        resA = [Res(f"A{c}") for c in range(32)]
        resB = [Res(f"B{c}") for c in range(64)]
        wp = [TT(st.enter_context(sbt(nc, f"wp{i}", [128, 32 * 512], BF16)), f"wp{i}") for i in range(2)]
        sg = [TT(st.enter_context(sbt(nc, f"sg{i}", [128, 512], F32)), f"sg{i}") for i in range(2)]
        slab = [TT(st.enter_context(sbt(nc, f"slab{i}", [128, 256], F32)), f"slab{i}") for i in range(3)]
        oslab = [TT(st.enter_context(sbt(nc, f"oslab{i}", [128, 256], F32)), f"oslab{i}") for i in range(3)]
        ss = TT(st.enter_context(sbt(nc, "ss", [128, 4], F32)), "ss")
        diag = [TT(st.enter_context(sbt(nc, f"diag{i}", [128, 128], F32)), f"diag{i}") for i in range(2)]
        pgu = [PT(st.enter_context(pst(nc, f"pgu{i}", [128, 512], F32)), f"pgu{i}") for i in range(4)]
        py = [PT(st.enter_context(pst(nc, f"py{i}", [128, 512], F32)), f"py{i}") for i in range(2)]
        pt = [PT(st.enter_context(pst(nc, f"pt{i}", [128, 512], F32)), f"pt{i}") for i in range(2)]
        L = Ctx()
        L.regA, L.regB, L.resA, L.resB, L.ss, L.diag, L.pt = regA, regB, resA, resB, ss, diag, pt
        L.wp, L.py, L.slab, L.oslab = wp, py, slab, oslab
        L.wi, L.yi, L.si = 0, 0, 0
        gi = 0
        first_bi = 1 if (last and s == 2) else 0
        if K.cfg.get("only_blocks") is not None:
            first_bi = K.cfg["only_blocks"][0]
        for bi, (t0, T) in enumerate(BLOCKS):
            r = 0 if bi == 0 else 1
            if bi == 0 and last and s == 2:
                continue
            if K.cfg.get("only_blocks") is not None and bi not in K.cfg["only_blocks"]:
                continue
            first_pass = (l == 0 and s == 0)
            src = K.xin if first_pass else K.xres
            front_end(K, L, src, first_pass, t0, T, l, s, r)
            nt = T // 128
            xnT = regA[:, :].rearrange("p (c t) -> p c t", c=32)
            hT = regB[:, :].rearrange("p (c t) -> p c t", c=64)
            parts = K.cfg.get("ffn_parts", "fgrd")
            for jp in range(32 if "g" in parts else 0):
                def ld(w, jp=jp):
                    wv = w.t[:, :].rearrange("p (k n) -> p k n", k=32)
                    for half in range(2):
                        src_w = w_gu[:, half * DFF + jp * 256: half * DFF + (jp + 1) * 256].rearrange(
                            "(kc p) n -> p kc n", p=128)
                        wload(K, wv[:, :, half * 256:(half + 1) * 256], w.res, src_w, K.cfg.get("wsplit", 4))
                w = wtile(K, L, jp, bi == first_bi, 16384, ld)
                wv = w.t[:, :].rearrange("p (k n) -> p k n", k=32)
                for jj in range(2):
                    j = jp * 2 + jj
                    pg_ = pgu[(gi % 2) * 2]
                    pu_ = pgu[(gi % 2) * 2 + 1]
                    sg_ = sg[gi % 2]
                    gi += 1
                    for kc in range(32):
                        S.mm(pg_[:, :T], V(wv[:, kc, jj * 128:(jj + 1) * 128], w.res),
                             V(xnT[:, kc, :T], resA[kc]), start=(kc == 0), stop=(kc == 31))
                    for kc in range(32):
                        S.mm(pu_[:, :T], V(wv[:, kc, 256 + jj * 128:256 + (jj + 1) * 128], w.res),
                             V(xnT[:, kc, :T], resA[kc]), start=(kc == 0), stop=(kc == 31))
                    S.act(sg_[:, :T], pg_[:, :T], AF.Silu)
                    S.tt("dve", V(hT[:, j, :T], resB[j]), sg_[:, :T], pu_[:, :T], ALU.mult)
            if "d" in parts:
                down_stage(K, L, l, s, r, t0, T, hT, resB, 64, w_dn, first_pass, 32, bi == first_bi)


def down_stage(K, L, l, s, r, t0, T, hT, resH, njc, w_dn, first_pass, tile0, wfirst):
    S = K.S
    nt = T // 128
    idx = (l * 3 + s) * 2 + r
    grow = L.regA[:, 0:8192].bitcast(F32)
    gsrc = K.gvec[idx * 32:(idx + 1) * 32, :].rearrange("a b -> (a b)").partition_broadcast(128)
    S.dma("sp", V(grow, L.resA), V(gsrc, K.gvec_res))
    for fb in range(16):
        nh = njc // 32

        def ld(w, fb=fb):
            wv = w.t[:, 0:njc * 256].rearrange("p (k n) -> p k n", k=njc)
            for half in range(nh):
                src_w = w_dn[half * 4096:(half + 1) * 4096, fb * 256:(fb + 1) * 256].rearrange(
                    "(jc p) n -> p jc n", p=128)
                wload(K, wv[:, half * 32:(half + 1) * 32, :], w.res, src_w, K.cfg.get("wsplit", 4))
        w = wtile(K, L, tile0 + fb, wfirst, njc * 256, ld)
        wv = w.t[:, 0:njc * 256].rearrange("p (k n) -> p k n", k=njc)
        for ti in range(nt):
            tile_i = t0 // 128 + ti
            p_ = L.py[L.yi % 2]
            L.yi += 1
            for jc in range(njc):
                S.mm(p_[:, :256], V(hT[:, jc, ti * 128:(ti + 1) * 128], resH[jc]), V(wv[:, jc, :], w.res),
                     start=(jc == 0), stop=(jc == njc - 1))
            sl = L.slab[L.si % 3]
            osl = L.oslab[L.si % 3]
            L.si += 1
            rows = slice(t0 + ti * 128, t0 + (ti + 1) * 128)
            cols = slice(fb * 256, (fb + 1) * 256)
            if first_pass:
                S.dma("sp", sl[:], V(K.xin[rows, cols], K.xin_res))
            else:
                S.dma("sp", sl[:], V(K.xres[rows, cols], K.xres_res[tile_i][fb]))
            S.tt("dve", osl[:], p_[:, :256], V(grow[:, cols], L.resA), ALU.mult)
            S.tt("pool", osl[:], osl[:], sl[:], ALU.add)
            S.dma("act", V(K.xres[rows, cols], K.xres_res[tile_i][fb]), osl[:])


def front_end(K, L, src, first_pass, t0, T, l, s, r):
    S = K.S
    nt = T // 128
    xnT = L.regA[:, :].rearrange("p (c t) -> p c t", c=32)
    idx = (l * 3 + s) * 2 + r
    for ti in range(nt):
        tile_i = t0 // 128 + ti
        xt = L.regB[:, (ti % 2) * 8192:(ti % 2 + 1) * 8192].bitcast(F32)
        xt_res = L.resB[(ti % 2) * 16:(ti % 2 + 1) * 16]
        junk = L.regB[:, 16384:16384 + 4096]
        junk_res = L.resB[32:40]
        rows = slice(t0 + ti * 128, t0 + (ti + 1) * 128)
        if first_pass:
            S.dma("sp", V(xt, xt_res), V(src[rows, :], K.xin_res))
        else:
            S.dma("sp", V(xt, xt_res), V(src[rows, :], K.xres_res[tile_i]))
        S.act(V(junk, junk_res), V(xt, xt_res), AF.Square, accum=L.ss[:, 0:1])
        S.act(L.ss[:, 1:2], L.ss[:, 0:1], AF.Ln, bias=K.consts[:, 0:1], scale=1.0 / D)
        S.act(L.ss[:, 2:3], L.ss[:, 1:2], AF.Exp, scale=-0.5)
        dg = L.diag[ti % 2]
        S.ts("dve", dg[:], K.ident[:], L.ss[:, 2:3], None, ALU.mult)
        for c4 in range(8):
            p_ = L.pt[c4 % 2]
            for q in range(4):
                c = c4 * 4 + q
                S.mm(p_[:, q * 128:(q + 1) * 128], V(xt[:, c * 128:(c + 1) * 128], xt_res), dg[:],
                     start=True, stop=True, sig=(q == 3))
            for q in range(4):
                c = c4 * 4 + q
                o = V(xnT[:, c, ti * 128:(ti + 1) * 128], L.resA[c])
                S.ts("dve", o, p_[:, q * 128:(q + 1) * 128], K.Amod[:, idx, c:c + 1], shiftv(K, l, s, r, c),
                     ALU.mult, ALU.add)


def dump_xres(K):
    nc, S = K.nc, K.S
    bb = [TT(nc.alloc_sbuf_tensor(f"dumpb{i}", [128, D], F32), f"dumpb{i}") for i in range(2)]
    for t in range(NXT // 128):
        b = bb[t % 2]
        S.dma("sp", b[:], V(K.xres[NCTX + t * 128:NCTX + (t + 1) * 128, :], K.xres_res[2 + t]))
        S.dma("sp", V(K.out[t * 128:(t + 1) * 128, :], K.out_res), b[:])


def dump_yT(K):
    nc, S = K.nc, K.S
    yb = [TT(nc.alloc_sbuf_tensor(f"dyb{i}", [128, NXT], BF16), f"dyb{i}") for i in range(2)]
    yf = [TT(nc.alloc_sbuf_tensor(f"dyf{i}", [128, NXT], F32), f"dyf{i}") for i in range(2)]
    o2 = K.out.rearrange("a (b t) -> (a b) t", t=NXT)
    which = K.cfg["dump_yT"]
    for c in range(32):
        b, f = yb[c % 2], yf[c % 2]
        if which == "x":
            S.dma("sp", b[:], V(K.yT[c * 128:(c + 1) * 128, NCTX:NTOK], K.yT_res[c]))
            S.copy("dve", f[:], b[:])
            S.dma("sp", V(o2[c * 128:(c + 1) * 128, :], K.out_res), f[:])
        else:
            S.dma("sp", b[:, 0:NCTX], V(K.yT[c * 128:(c + 1) * 128, 0:NCTX], K.yT_res[c]))
            S.copy("dve", f[:, 0:NCTX], b[:, 0:NCTX])
            S.dma("sp", V(o2[c * 128:(c + 1) * 128, 0:NCTX], K.out_res), f[:, 0:NCTX])


def phase_final(K):
    nc, S = K.nc, K.S
    from contextlib import ExitStack
    with ExitStack() as st:
        nf = TT(st.enter_context(sbt(nc, "nf", [128, D], F32)), "nf")
        xt = [TT(st.enter_context(sbt(nc, f"fx{i}", [128, D], F32)), f"fx{i}") for i in range(2)]
        yo = [TT(st.enter_context(sbt(nc, f"fy{i}", [128, D], F32)), f"fy{i}") for i in range(2)]
        junk = TT(st.enter_context(sbt(nc, "fjunk", [128, D], BF16)), "fjunk")
        ss = TT(st.enter_context(sbt(nc, "fss", [128, 4], F32)), "fss")
        S.dma("sp", nf[:], V(K.nfin, K.ext))
        for t in range(NXT // 128):
            x_ = xt[t % 2]
            y_ = yo[t % 2]
            tile_i = 2 + t
            S.dma("sp", x_[:], V(K.xres[NCTX + t * 128:NCTX + (t + 1) * 128, :], K.xres_res[tile_i]))
            S.act(junk[:], x_[:], AF.Square, accum=ss[:, 0:1])
            S.act(ss[:, 1:2], ss[:, 0:1], AF.Ln, bias=K.consts[:, 0:1], scale=1.0 / D)
            S.act(ss[:, 2:3], ss[:, 1:2], AF.Exp, scale=-0.5)
            S.stt("dve", y_[:], x_[:], ss[:, 2:3], nf[:], ALU.mult, ALU.mult)
            S.dma("act", V(K.out[t * 128:(t + 1) * 128, :], K.out_res), y_[:])


def fm(v):
    v = np.asarray(v, np.float32)
    return np.ascontiguousarray(v.reshape(-1, 128).T)


def host_inputs(inp, b):
    m = {}
    m["xin"] = np.ascontiguousarray(np.concatenate([inp["ctx"][b], inp["x"][b]], axis=0), dtype=np.float32)
    cv = np.stack([fm(inp["c_ctx"]), fm(inp["c"][b])], axis=-1)
    m["cv"] = np.ascontiguousarray(cv.reshape(128, 64))
    m["bm"] = np.ascontiguousarray(np.concatenate([fm(inp["b_mod"][l]) for l in range(DEPTH)], axis=1))
    m["nrm"] = np.ascontiguousarray(np.concatenate(
        [fm(inp[n][l]) for l in range(DEPTH) for n in ("norm_ffn1", "norm_mix", "norm_ffn2")], axis=1))
    for n in ("w_mod", "w_ffn1_gu", "w_ffn2_gu", "w_ffn1_down", "w_ffn2_down", "w_in", "w_out"):
        for l in range(DEPTH):
            m[f"{n}_{l}"] = inp[n][l]
    m["ident"] = np.eye(128, dtype=np.float32)
    perm = np.array([d * 32 + t * 16 + h for t in range(2) for d in range(2) for h in range(16)])
    gpar = np.zeros((64, 2 * DEPTH), np.float32)
    for l in range(DEPTH):
        m[f"w_ba_{l}"] = np.ascontiguousarray(inp["w_in"][l][:, GBA:GBA + 64][:, perm])
        gpar[32:64, 2 * l] = np.asarray(inp["a_log"][l], np.float32).reshape(32)
        gpar[32:64, 2 * l + 1] = np.asarray(inp["dt_bias"][l], np.float32).reshape(32)
        m[f"pool_w_{l}"] = np.ascontiguousarray(np.asarray(inp["pool_w"][l], np.float32).reshape(1024, 256))
    m["gpar"] = gpar
    m["cq"] = np.ascontiguousarray(np.concatenate(
        [fm(inp["conv_qkv"][l][j]) for l in range(DEPTH) for j in range(5)], axis=1))
    m["gnw"] = np.ascontiguousarray(np.stack([np.asarray(inp["gdn_norm"][l], np.float32) for l in range(DEPTH)], axis=1))
    m["csh"] = np.ascontiguousarray(np.concatenate(
        [fm(inp["conv_short"][l][j]) for l in range(DEPTH) for j in range(3)], axis=1))
    m["psc"] = np.ascontiguousarray(np.concatenate([fm(inp["pool_scale"][l]) for l in range(DEPTH)], axis=1))
    ii = np.arange(64)
    U = (ii[:, None] <= ii[None, :]).astype(np.float32)
    Us = (ii[:, None] < ii[None, :]).astype(np.float32)
    m["msk"] = np.ascontiguousarray(np.concatenate([U, U.T, Us, Us.T], axis=1))
    m["pcnt"] = pool_counts()
    m["nfin"] = np.ascontiguousarray(np.broadcast_to(np.asarray(inp["norm_final"], np.float32)[None, :], (128, D)))
    return m


def pool_counts():
    out = np.zeros((4, 128, NTOK), np.float32)
    for wi, w in enumerate((2, 4, 8, 16)):
        for (n, rep, base) in ((NCTX, 1, 0), (NXT // 64, 64, NCTX)):
            t = np.arange(n)
            lo = np.clip(t - w // 2, 0, n)
            hi = np.clip(t - w // 2 + w, 0, n)
            inv = np.repeat(np.float32(1.0) / (hi - lo).astype(np.float32), rep)
            out[wi, :, base:base + n * rep] = inv[None, :]
    return np.ascontiguousarray(out.reshape(4 * 128, NTOK))


_CACHE = {}


def kernel(**inputs):
    inp = {k: np.asarray(v) for k, v in inputs.items()}
    n_cores = 4
    if "nc" not in _CACHE:
        _CACHE["nc"] = build_program({})
        _CACHE["used"] = _CACHE["nc"].used_inputs
    nc = _CACHE["nc"]
    in_maps = [host_inputs(inp, b) for b in range(n_cores)]
    used = set(_CACHE["used"])
    in_maps = [{k: v for k, v in m.items() if k in used} for m in in_maps]
    res = run_bass_kernel_spmd(nc, in_maps, core_ids=list(range(n_cores)))
    out = np.stack([res.results[b]["out"] for b in range(n_cores)], axis=0)
    return out.astype(np.float32)


GQ, GK, GV, GZ, GBA, GSX, GSB, GSC, GPL = 0, 1024, 2048, 4096, 6144, 6208, 7232, 8256, 9280
SEQS = [(0, NCTX), (NCTX, NXT)]


def mixer_setup(K):
    nc = K.nc
    if K.cfg.get("pT_input"):
        K.pT = dram_in(nc, "pT", [D_IN, NTOK])
    else:
        K.pT = nc.dram_tensor("pT", [D_IN, NTOK], F32).ap()
    K.yT = nc.dram_tensor("yT", [D, NTOK], BF16).ap()
    K.pT_res = [[Res(f"pT{c}_{b}") for b in range(len(BLOCKS))] for c in range(81)]
    K.yT_res = [Res(f"yT{c}") for c in range(32)]
    K.w_ba = [dram_in(nc, f"w_ba_{l}", [D, 64]) for l in range(DEPTH)]
    K.gpar = dram_in(nc, "gpar", [64, 2 * DEPTH])
    K.cq = dram_in(nc, "cq", [128, DEPTH * 5 * 32])
    K.gnw = dram_in(nc, "gnw", [128, DEPTH])
    K.csh = dram_in(nc, "csh", [128, DEPTH * 3 * 8])
    K.psc = dram_in(nc, "psc", [128, DEPTH * 8])
    K.msk = dram_in(nc, "msk", [64, 4 * 64])
    K.pcnt = dram_in(nc, "pcnt", [4 * 128, NTOK])
    K.extra_inputs = [f"w_ba_{l}" for l in range(DEPTH)] + ["gpar", "cq", "gnw", "csh", "psc", "msk", "pcnt"]


def phase_inproj(K, l):
    nc, S = K.nc, K.S
    from contextlib import ExitStack
    w_in = W(K, "w_in", l)
    with ExitStack() as st:
        regA = st.enter_context(sbt(nc, "regA", [128, 32 * 512], BF16))
        regB = st.enter_context(sbt(nc, "regB", [128, 64 * 512], BF16))
        L = Ctx()
        L.regA, L.regB = regA, regB
        L.resA = [Res(f"A{c}") for c in range(32)]
        L.resB = [Res(f"B{c}") for c in range(64)]
        L.ss = TT(st.enter_context(sbt(nc, "ss", [128, 4], F32)), "ss")
        L.diag = [TT(st.enter_context(sbt(nc, f"diag{i}", [128, 128], F32)), f"diag{i}") for i in range(2)]
        L.pt = [PT(st.enter_context(pst(nc, f"pt{i}", [128, 512], F32)), f"pt{i}") for i in range(2)]
        wp = [TT(st.enter_context(sbt(nc, f"wp{i}", [128, 32 * 512], BF16)), f"wp{i}") for i in range(2)]
        L.wp, L.wi = wp, 0
        stg = [TT(st.enter_context(sbt(nc, f"stg{i}", [128, 512], F32)), f"stg{i}") for i in range(4)]
        pp = [PT(st.enter_context(pst(nc, f"pp{i}", [128, 512], F32)), f"pp{i}") for i in range(4)]
        xnT = regA[:, :].rearrange("p (c t) -> p c t", c=32)
        wi = 0
        oi = 0
        for bi, (t0, T) in enumerate(BLOCKS):
            r = 0 if bi == 0 else 1
            front_end(K, L, K.xres, False, t0, T, l, 1, r)
            for wt in range(21):
                if wt < 20:
                    c0 = wt * 512 if wt < 12 else wt * 512 + 64

                    def ld(w, c0=c0):
                        wv = w.t[:, :].rearrange("p (k n) -> p k n", k=32)
                        wload(K, wv[:, :, :], w.res, w_in[:, c0:c0 + 512].rearrange("(kc p) n -> p kc n", p=128), 4)
                    chunks = [(q * 128, 128, (c0 - (0 if wt < 12 else 64)) // 128 + q, c0 + q * 128) for q in range(4)]
                else:
                    def ld(w):
                        wv = w.t[:, :].rearrange("p (k n) -> p k n", k=32)
                        wload(K, wv[:, :, 0:64], w.res, K.w_ba[l].rearrange("(kc p) n -> p kc n", p=128), 1)
                    chunks = [(0, 64, 80, GBA)]
                w = wtile(K, L, wt, bi == 0, 16384, ld)
                wv = w.t[:, :].rearrange("p (k n) -> p k n", k=32)
                for (off, m, ci, row0) in chunks:
                    p_ = pp[oi % 4]
                    s_ = stg[oi % 4]
                    for kc in range(32):
                        S.mm(p_[0:m, :T], V(wv[:, kc, off:off + m], w.res), V(xnT[:, kc, :T], L.resA[kc]),
                             start=(kc == 0), stop=(kc == 31))
                    if oi % 2 == 0:
                        S.copy("act", s_[0:m, :T], p_[0:m, :T])
                    else:
                        S.copy("dve", s_[0:m, :T], p_[0:m, :T])
                    S.dma("sp", V(K.pT[row0:row0 + m, t0:t0 + T], K.pT_res[ci][bi]), s_[0:m, :T])
                    oi += 1


def run_gens(gens, max_rounds=10 ** 9):
    gens = list(gens)
    rounds = 0
    while gens and rounds < max_rounds:
        rounds += 1
        nxt = []
        for g in gens:
            try:
                next(g)
                nxt.append(g)
            except StopIteration:
                pass
        gens = nxt


def chain_order(d):
    if d == 0:
        return list(range(36))
    return [3, 2, 1, 0] + list(range(35, 3, -1))


def phase_gdn(K, l):
    nc, S = K.nc, K.S
    from contextlib import ExitStack
    NB = 4
    K._gb = {}
    with ExitStack() as st:
        def sb(name, shape, dt=F32):
            return TT(st.enter_context(sbt(nc, name, shape, dt)), name)
        ps_tiles = [PT(st.enter_context(pst(nc, f"gps{i}", [128, 512], F32)), f"gps{i}") for i in range(8)]
        ps_i = [0]

        def ps():
            t = ps_tiles[ps_i[0] % 8]
            ps_i[0] += 1
            return t
        msk = sb("mskt", [64, 4, 64])
        S.dma("sp", V(msk.t[:].rearrange("p a b -> p (a b)"), msk.res), V(K.msk, K.ext))
        U, Lo, Us, Los = (msk[:, i, :] for i in range(4))
        cq = sb("cqt", [128, 5, 32])
        S.dma("sp", V(cq.t[:].rearrange("p a b -> p (a b)"), cq.res),
              V(K.cq[:, l * 160:(l + 1) * 160], K.ext))
        gpar = sb("gpart", [64, 2])
        S.dma("sp", gpar[:], V(K.gpar[:, 2 * l:2 * l + 2], K.ext))
        gnw_all = sb("gnwt", [128, DEPTH])
        S.dma("sp", gnw_all[:], V(K.gnw, K.ext))
        gnw = gnw_all[:, l:l + 1]
        baT = sb("baT", [64, NTOK])
        S.dma("sp", baT[:], V(K.pT[GBA:GBA + 64, :], [K.pT_res[80][b] for b in range(len(BLOCKS))]))
        bg = sb("bg", [64, 36, 64])
        negA = sb("negA", [64, 1])
        S.act(negA[32:64, :], gpar[32:64, 0:1], AF.Exp)
        S.ts("dve", negA[32:64, :], negA[32:64, :], -1.0, None, ALU.mult)
        S.act(baT[0:32, :], baT[0:32, :], AF.Sigmoid)
        S.ts("dve", baT[32:64, :], baT[32:64, :], gpar[32:64, 1:2], None, ALU.add)
        S.act(baT[32:64, :], baT[32:64, :], AF.Exp)
        S.ts("dve", baT[32:64, :], baT[32:64, :], 1.0, None, ALU.add)
        S.act(baT[32:64, :], baT[32:64, :], AF.Ln)
        S.ts("dve", baT[32:64, :], baT[32:64, :], negA[32:64, 0:1], None, ALU.mult)
        for c in range(36):
            p_ = ps()
            S.transpose(p_[0:64, 0:64], baT[:, c * 64:(c + 1) * 64], K.ident[0:64, 0:64])
            S.copy("dve" if c % 2 else "act", bg[:, c, :], p_[0:64, 0:64])
        if K.cfg.get("gdn_stop", 9) <= 1:
            return
        qT = sb("qT", [128, NTOK])
        kT = sb("kT", [128, NTOK])
        vT = sb("vT", [128, 2, NTOK])
        raw = [sb(f"raw{i}", [128, NTOK + 8]) for i in range(1)]
        sq = sb("sq", [128, NTOK])
        oacc = sb("oacc", [64, 36, 2, 128])
        Sst = sb("Sst", [128, 2, 128])
        ri = [0]

        def load_conv(dst, row0, chan_chunk):
            rw = raw[0]
            ri[0] += 1
            S.memset("pool", rw[:], 0.0)
            ci = row0 // 128
            for bi, (t0, T) in enumerate(BLOCKS):
                off = 2 if bi == 0 else 6
                S.dma("sp", rw[:, off + t0:off + t0 + T], V(K.pT[row0:row0 + 128, t0:t0 + T], K.pT_res[ci][bi]))
            for (s0, n) in SEQS:
                off = 2 if s0 == 0 else 6
                o = dst[:, s0:s0 + n] if isinstance(dst, TT) else V(dst.ap[:, s0:s0 + n], dst.res)
                e = "dve"
                for j in range(5):
                    src = rw[:, off + s0 + j - 2: off + s0 + j - 2 + n]
                    wj = cq[:, j, chan_chunk:chan_chunk + 1]
                    if j == 0:
                        S.ts(e, o, src, wj, None, ALU.mult)
                    else:
                        S.stt(e, o, src, wj, o, ALU.mult, ALU.add)
            full = dst[:, :] if isinstance(dst, TT) else dst
            S.act(full, full, AF.Silu)

        def l2n(t, scale):
            S.act(sq[:], t[:], AF.Square)
            for c0 in range(0, NTOK, 512):
                n = min(512, NTOK - c0)
                p_ = ps()
                S.mm(p_[:, :n], K.ones[:, :], sq[:, c0:c0 + n])
                S.ts("dve", sq[:, c0:c0 + n], p_[:, :n], EPS, None, ALU.add)
                S.act(sq[:, c0:c0 + n], sq[:, c0:c0 + n], AF.Ln)
                S.act(sq[:, c0:c0 + n], sq[:, c0:c0 + n], AF.Exp, scale=-0.5)
                S.stt("dve", t[:, c0:c0 + n], t[:, c0:c0 + n], scale, sq[:, c0:c0 + n], ALU.mult, ALU.mult)

        for g in range(K.cfg.get("n_groups", 8)):
            load_conv(qT, GQ + g * 128, g)
            load_conv(kT, GK + g * 128, 8 + g)
            for hh in range(2):
                load_conv(V(vT.t[:, hh, :], vT.res), GV + (2 * g + hh) * 128, 16 + 2 * g + hh)
            l2n(qT, 128 ** -0.5)
            l2n(kT, 1.0)
            if K.cfg.get("gdn_stop", 9) <= 2:
                continue
            for d in range(2):
                Minc, MsT = (U, Los) if d == 0 else (Lo, Us)
                mS_ij, mI_ji = (Los, U) if d == 0 else (Us, Lo)
                order = chain_order(d)
                bufs = [dict() for _ in range(NB)]
                for b in range(NB):
                    B = bufs[b]
                    for nm, shp in (("gInc", [64, 2, 64]), ("gMs", [64, 2, 64]), ("Dm", [64, 2, 64]), ("DT", [64, 2, 64]),
                                    ("KKm", [64, 64]), ("QKm", [64, 64]), ("P0", [64, 2, 64]), ("PT0", [64, 2, 64]),
                                    ("P1", [64, 2, 64]), ("PT1", [64, 2, 64]), ("X0", [64, 2, 64]), ("X1", [64, 2, 64]),
                                    ("kbg", [64, 2, 128]), ("kdec", [64, 2, 128]), ("vb", [64, 2, 128]),
                                    ("u", [64, 2, 128]), ("wTn", [128, 2, 64]), ("qdT", [128, 2, 64]),
                                    ("attnT", [64, 2, 64]), ("egl", [128, 2]), ("sc", [64, 8]), ("ktok", [64, 128])):
                        key = f"g_{b}_{nm}"
                        if key not in K._gb:
                            K._gb[key] = sb(key, shp)
                        B[nm] = K._gb[key]

                def pre(c, B, d=d, Minc=Minc, MsT=MsT, mS_ij=mS_ij, mI_ji=mI_ji):
                    tok = slice(c * 64, (c + 1) * 64)
                    beta = bg[:, c, d * 16 + 2 * g: d * 16 + 2 * g + 2]
                    gg = bg[:, c, 32 + d * 16 + 2 * g: 32 + d * 16 + 2 * g + 2]
                    sc = B["sc"]
                    for ln in range(2):
                        S.ts("pool", B["gInc"][:, ln, :], Minc, V(gg.ap[:, ln:ln + 1], gg.res), None, ALU.mult)
                        S.ts("pool", B["gMs"][:, ln, :], MsT, V(gg.ap[:, ln:ln + 1], gg.res), None, ALU.mult)
                    pd = ps()
                    for ln in range(2):
                        S.mm(pd[0:64, ln * 64:(ln + 1) * 64], Minc, B["gMs"][:, ln, :], sig=False)
                        S.mm(pd[0:64, 128 + ln * 64:128 + (ln + 1) * 64], MsT, B["gInc"][:, ln, :], sig=False)
                    S.mm(pd[0:64, 256:258], Minc, gg, sig=False)
                    S.mm(pd[:, 260:262], K.ones[0:64, :], gg)
                    pr = ps()
                    for ln in range(2):
                        S.mm(pr[:, ln * 64:(ln + 1) * 64], K.ones[0:64, :], B["gInc"][:, ln, :], sig=(ln == 1))
                    yield
                    S.act(V(B["Dm"].t[:].rearrange("p a b -> p (a b)"), B["Dm"].res), pd[0:64, 0:128], AF.Exp)
                    S.act(V(B["DT"].t[:].rearrange("p a b -> p (a b)"), B["DT"].res), pd[0:64, 128:256], AF.Exp)
                    S.act(V(B["qdT"].t[:].rearrange("p a b -> p (a b)"), B["qdT"].res), pr[:, 0:128], AF.Exp)
                    S.act(B["egl"][:], pd[:, 260:262], AF.Exp)
                    S.act(sc[:, 0:2], pd[0:64, 256:258], AF.Exp)
                    S.ts("dve", sc[:, 2:4], pd[0:64, 256:258], -1.0, None, ALU.mult)
                    S.tt("dve", sc[:, 2:4], sc[:, 2:4], pd[0:64, 260:262], ALU.add)
                    S.act(sc[:, 2:4], sc[:, 2:4], AF.Exp)
                    S.tt("dve", sc[:, 4:6], sc[:, 0:2], beta, ALU.mult)
                    S.ts("dve", sc[:, 6:8], beta, -1.0, None, ALU.mult)
                    for ln in range(2):
                        S.tt("pool", B["qdT"][:, ln, :], B["qdT"][:, ln, :], qT[:, tok], ALU.mult)
                    pg_ = ps()
                    S.mm(pg_[0:64, 0:64], kT[:, tok], kT[:, tok], sig=False)
                    S.mm(pg_[0:64, 64:128], kT[:, tok], qT[:, tok], sig=False)
                    S.transpose(pg_[0:64, 128:256], kT[:, tok], K.ident[:, :], sig=False)
                    S.transpose(pg_[0:64, 256:384], vT[:, 0, tok], K.ident[:, :], sig=False)
                    S.transpose(pg_[0:64, 384:512], vT[:, 1, tok], K.ident[:, :])
                    yield
                    S.tt("dve", B["KKm"][:], pg_[0:64, 0:64], mS_ij, ALU.mult)
                    S.tt("dve", B["QKm"][:], pg_[0:64, 64:128], mI_ji, ALU.mult)
                    S.copy("act", B["ktok"][:], pg_[0:64, 128:256])
                    for ln in range(2):
                        S.ts("dve", B["vb"][:, ln, :], pg_[0:64, 256 + ln * 128:384 + ln * 128], V(beta.ap[:, ln:ln + 1], beta.res), None, ALU.mult)
                        S.ts("pool", B["kbg"][:, ln, :], B["ktok"][:], sc[:, 4 + ln:5 + ln], None, ALU.mult)
                        S.ts("pool", B["kdec"][:, ln, :], B["ktok"][:], sc[:, 2 + ln:3 + ln], None, ALU.mult)
                        S.stt("dve", B["P0"][:, ln, :], B["KKm"][:], sc[:, 6 + ln:7 + ln], B["Dm"][:, ln, :], ALU.mult, ALU.mult)
                        S.tt("pool", B["attnT"][:, ln, :], B["QKm"][:], B["DT"][:, ln, :], ALU.mult)
                    yield
                    pn = ps()
                    for ln in range(2):
                        S.transpose(pn[0:64, ln * 64:(ln + 1) * 64], B["P0"][:, ln, :], K.ident[0:64, 0:64], sig=(ln == 1))
                    yield
                    var = K.cfg.get("r5", 3)
                    S.copy("dve" if var == 1 else "act", V(B["PT0"].t[:].rearrange("p a b -> p (a b)"), B["PT0"].res), pn[0:64, 0:128])
                    for ln in range(2):
                        if var == 2:
                            pass
                        elif var == 3:
                            S.tt("dve", B["X0"][:, ln, :], B["PT0"][:, ln, :], K.ident[0:64, 0:64], ALU.add)
                        else:
                            S.tt("dve", B["X0"][:, ln, :], pn[0:64, ln * 64:(ln + 1) * 64], K.ident[0:64, 0:64], ALU.add)
                    yield
                    cur, nxt = ("P0", "PT0", "X0"), ("P1", "PT1", "X1")
                    for lvl in range(5):
                        Pc, PTc, Xc = (B[n] for n in cur)
                        Pn, PTn, Xn = (B[n] for n in nxt)
                        p2 = ps()
                        for ln in range(2):
                            S.mm(p2[0:64, ln * 64:(ln + 1) * 64], PTc[:, ln, :], Pc[:, ln, :], sig=False)
                            S.mm(p2[0:64, 128 + ln * 64:128 + (ln + 1) * 64], Pc[:, ln, :], PTc[:, ln, :], sig=(ln == 1))
                        yield
                        S.copy("act", V(Pn.t[:].rearrange("p a b -> p (a b)"), Pn.res), p2[0:64, 0:128])
                        if lvl < 4:
                            S.copy("dve", V(PTn.t[:].rearrange("p a b -> p (a b)"), PTn.res), p2[0:64, 128:256])
                        px = ps()
                        for ln in range(2):
                            S.mm(px[0:64, ln * 64:(ln + 1) * 64], K.ident[0:64, 0:64], Xc[:, ln, :], start=True, stop=False, sig=False)
                            S.mm(px[0:64, ln * 64:(ln + 1) * 64], Pn[:, ln, :], Xc[:, ln, :], start=False, stop=True, sig=(ln == 1))
                        yield
                        S.copy("dve", V(Xn.t[:].rearrange("p a b -> p (a b)"), Xn.res), px[0:64, 0:128])
                        cur, nxt = nxt, cur
                    Xf = B[cur[2]]
                    pu = ps()
                    for ln in range(2):
                        S.mm(pu[0:64, ln * 128:(ln + 1) * 128], Xf[:, ln, :], B["vb"][:, ln, :], sig=False)
                        S.mm(pu[:, 256 + ln * 64:256 + (ln + 1) * 64], B["kbg"][:, ln, :], Xf[:, ln, :], sig=(ln == 1))
                    yield
                    S.copy("act", V(B["u"].t[:].rearrange("p a b -> p (a b)"), B["u"].res), pu[0:64, 0:256])
                    S.ts("dve", V(B["wTn"].t[:].rearrange("p a b -> p (a b)"), B["wTn"].res), pu[:, 256:384], -1.0, None, ALU.mult)

                if "vnew0" not in K._gb:
                    for i in range(2):
                        K._gb[f"vnew{i}"] = sb(f"vnew{i}", [64, 2, 128])
                vnew = [K._gb["vnew0"], K._gb["vnew1"]]

                def steps(chunks, bset, si0, d=d):
                    for i, c in enumerate(chunks):
                        B = bset[i]
                        vn = vnew[(si0 + i) % 2]
                        pv = ps()
                        for ln in range(2):
                            S.mm(pv[0:64, ln * 128:(ln + 1) * 128], B["wTn"][:, ln, :], Sst[:, ln, :], sig=(ln == 1))
                        yield
                        S.tt("dve", V(vn.t[:].rearrange("p a b -> p (a b)"), vn.res), pv[0:64, 0:256],
                             V(B["u"].t[:].rearrange("p a b -> p (a b)"), B["u"].res), ALU.add)
                        po = ps()
                        for ln in range(2):
                            S.mm(po[0:64, ln * 128:(ln + 1) * 128], B["qdT"][:, ln, :], Sst[:, ln, :],
                                 start=True, stop=False, sig=False)
                            S.mm(po[0:64, ln * 128:(ln + 1) * 128], B["attnT"][:, ln, :], vn[:, ln, :],
                                 start=False, stop=True, sig=False)
                            S.mm(po[:, 256 + ln * 128:256 + (ln + 1) * 128], B["kdec"][:, ln, :], vn[:, ln, :],
                                 sig=(ln == 1))
                        yield
                        oc = V(oacc.t[:, c, :, :].rearrange("p a b -> p (a b)"), oacc.res)
                        if d == 0:
                            S.copy("act", oc, po[0:64, 0:256])
                        else:
                            S.tt("dve", oc, po[0:64, 0:256], oc, ALU.add)
                        for ln in range(2):
                            S.stt("dve", Sst[:, ln, :], Sst[:, ln, :], B["egl"][:, ln:ln + 1],
                                  po[:, 256 + ln * 128:256 + (ln + 1) * 128], ALU.mult, ALU.add)

                S.memset("dve", V(Sst.t[:].rearrange("p a b -> p (a b)"), Sst.res), 0.0)
                nsteps = K.cfg.get("n_steps", 36)
                order = order[:nsteps]
                batches = [order[i:i + 2] for i in range(0, len(order), 2)]
                run_gens([pre(c, bufs[i]) for i, c in enumerate(batches[0])], K.cfg.get("pre_rounds", 10 ** 9))
                if K.cfg.get("gdn_stop", 9) <= 3:
                    continue
                for bi_, batch in enumerate(batches):
                    cur = bufs[0:2] if bi_ % 2 == 0 else bufs[2:4]
                    oth = bufs[2:4] if bi_ % 2 == 0 else bufs[0:2]
                    gens = [steps(batch, cur, bi_ * 2)]
                    if bi_ + 1 < len(batches):
                        gens += [pre(c, oth[i]) for i, c in enumerate(batches[bi_ + 1])]
                    run_gens(gens)
            if K.cfg.get("gdn_stop", 9) >= 5:
                gdn_out(K, l, g, oacc, gnw, ps, sb)


def gdn_out(K, l, g, oacc, gnw, ps, sb):
    S = K.S
    if "go_z" not in K._gb:
        K._gb["go_z"] = [sb(f"go_z{i}", [128, 2, 512]) for i in range(2)]
        K._gb["go_y"] = [sb(f"go_y{i}", [128, 2, 512], BF16) for i in range(2)]
        K._gb["go_s"] = sb("go_s", [64, 36, 2, 2])
        K._gb["go_j"] = sb("go_j", [64, 128])
        K._gb["go_o"] = [sb(f"go_o{i}", [64, 2, 128]) for i in range(2)]
    ss = K._gb["go_s"]
    for c in range(36):
        for ln in range(2):
            S.act(K._gb["go_j"][:], oacc[:, c, ln, :], AF.Square, accum=ss[:, c, ln, 0:1])
    ssf = V(ss.t[:, :, :, 0], ss.res)
    rsf = V(ss.t[:, :, :, 1], ss.res)
    S.act(rsf, ssf, AF.Ln, bias=K.consts[0:64, 0:1], scale=1.0 / 128)
    S.act(rsf, rsf, AF.Exp, scale=-0.5)
    for bi, (t0, T) in enumerate(BLOCKS):
        z = K._gb["go_z"][bi % 2]
        y = K._gb["go_y"][bi % 2]
        for ln in range(2):
            h = 2 * g + ln
            S.dma("sp", z[:, ln, :T], V(K.pT[GZ + h * 128:GZ + (h + 1) * 128, t0:t0 + T], K.pT_res[(GZ // 128) + h][bi]))
        S.act(V(z.t[:, :, :T], z.res), V(z.t[:, :, :T], z.res), AF.Silu)
        for ci in range(T // 64):
            c = t0 // 64 + ci
            o_ = K._gb["go_o"][ci % 2]
            p_ = ps()
            for ln in range(2):
                S.ts("pool", o_[:, ln, :], oacc[:, c, ln, :], ss[:, c, ln, 1:2], None, ALU.mult)
            for ln in range(2):
                S.transpose(p_[:, ln * 64:(ln + 1) * 64], o_[:, ln, :], K.ident[0:64, 0:64], sig=(ln == 1))
            for ln in range(2):
                S.stt("dve", y[:, ln, ci * 64:(ci + 1) * 64], p_[:, ln * 64:(ln + 1) * 64], gnw,
                      z[:, ln, ci * 64:(ci + 1) * 64], ALU.mult, ALU.mult)
        for ln in range(2):
            h = 2 * g + ln
            S.dma("act", V(K.yT[h * 128:(h + 1) * 128, t0:t0 + T], K.yT_res[h]), y[:, ln, :T])


def pT_res_rows(K, row0):
    ci = row0 // 128 if row0 < GBA else (row0 - 64) // 128
    return K.pT_res[ci]


def phase_sc_pool(K, l):
    nc, S = K.nc, K.S
    from contextlib import ExitStack
    with ExitStack() as st:
        def sb(name, shape, dt=F32):
            return TT(st.enter_context(sbt(nc, name, shape, dt)), name)
        csh = sb("csht", [128, 3, 8])
        S.dma("sp", V(csh.t[:].rearrange("p a b -> p (a b)"), csh.res), V(K.csh[:, l * 24:(l + 1) * 24], K.ext))
        psc = sb("psct", [128, 8])
        S.dma("sp", psc[:], V(K.psc[:, l * 8:(l + 1) * 8], K.ext))
        xin = [sb(f"scx{i}", [128, NTOK]) for i in range(2)]
        bgt = [sb(f"scb{i}", [128, NTOK]) for i in range(2)]
        cgt = [sb(f"scc{i}", [128, NTOK]) for i in range(2)]
        acc = [sb(f"sca{i}", [128, NTOK]) for i in range(2)]
        yb = [sb(f"scy{i}", [128, NTOK], BF16) for i in range(2)]

        def load(dst, row0):
            rr = pT_res_rows(K, row0)
            for bi, (t0, T) in enumerate(BLOCKS):
                S.dma("sp", dst[:, t0:t0 + T], V(K.pT[row0:row0 + 128, t0:t0 + T], rr[bi]))

        for cc in range(8):
            x_, b_, c_, a_, y_ = xin[cc % 2], bgt[cc % 2], cgt[cc % 2], acc[cc % 2], yb[cc % 2]
            load(x_, GSX + cc * 128)
            load(b_, GSB + cc * 128)
            load(c_, GSC + cc * 128)
            S.tt("pool", x_[:], x_[:], c_[:], ALU.mult)
            S.ts("dve", a_[:], x_[:], csh[:, 1, cc:cc + 1], None, ALU.mult)
            for (s0, n, wdt) in ((0, NCTX, NCTX), (NCTX, NXT, 64)):
                v3 = x_.t[:, s0:s0 + n].rearrange("p (r w) -> p r w", w=wdt)
                a3 = a_.t[:, s0:s0 + n].rearrange("p (r w) -> p r w", w=wdt)
                S.stt("dve", V(a3[:, :, 1:], a_.res), V(v3[:, :, :wdt - 1], x_.res), csh[:, 0, cc:cc + 1],
                      V(a3[:, :, 1:], a_.res), ALU.mult, ALU.add)
                S.stt("dve", V(a3[:, :, :wdt - 1], a_.res), V(v3[:, :, 1:], x_.res), csh[:, 2, cc:cc + 1],
                      V(a3[:, :, :wdt - 1], a_.res), ALU.mult, ALU.add)
            S.tt("pool", y_[:], a_[:], b_[:], ALU.mult)
            S.dma("act", V(K.yT[2048 + cc * 128:2048 + (cc + 1) * 128, :], K.yT_res[16 + cc]), y_[:])
        pw = W(K, "pool_w", l)
        pwt = sb("pwt", [128, 2, 256], BF16)
        cnt = sb("pcntt", [128, NTOK])
        dfb = [sb(f"pdf{i}", [128, NTOK], BF16) for i in range(2)]
        PADX = NTOK + 16 * 64 + 16
        bufA = sb("plA", [128, PADX])
        bufB = sb("plB", [128, PADX])
        pps = [PT(st.enter_context(pst(nc, f"plp{i}", [128, 512], F32)), f"plp{i}") for i in range(2)]
        XO = NCTX + 16
        for gi_ in range(4):
            win = (2, 4, 8, 16)[gi_]
            nlev = gi_ + 1
            S.dma("sp", cnt[:], V(K.pcnt[gi_ * 128:(gi_ + 1) * 128, :], K.ext))
            S.dma("pool", V(pwt.t[:, :, :], pwt.res),
                  V(pw[gi_ * 256:(gi_ + 1) * 256, :].rearrange("(c p) n -> p c n", p=128), K.ext))
            for c2 in range(2):
                pc = gi_ * 2 + c2
                u_ = xin[c2]
                load(u_, GPL + pc * 128)
                S.memset("pool", bufA[:], 0.0)
                S.copy("pool", bufA[:, 8:8 + NCTX], u_[:, 0:NCTX])
                S.copy("pool", bufA[:, XO + 8 * 64:XO + 40 * 64], u_[:, NCTX:NTOK])
                src, dst = bufA, bufB
                for lev in range(nlev):
                    sh = 1 if lev == 0 else 2 ** (lev - 1)
                    lo = (1, 2, 4, 8)[lev]
                    for (base, n, wdt) in ((0, NCTX, 1), (XO, NXT // 64, 64)):
                        hi = n + 16 - lo + (1 if lev == 0 else 0)
                        hi = min(hi, n + 16 - (0 if lev == 0 else sh))
                        a0 = src.t[:, base + (lo - sh) * wdt: base + (hi - sh) * wdt]
                        if lev == 0:
                            a1 = src.t[:, base + lo * wdt: base + hi * wdt]
                        else:
                            a1 = src.t[:, base + (lo + sh) * wdt: base + (hi + sh) * wdt]
                        S.tt("dve", V(dst.t[:, base + lo * wdt: base + hi * wdt], dst.res), V(a0, src.res),
                             V(a1, src.res), ALU.add)
                    src, dst = dst, src
                S.tt("dve", a := acc[c2][:, 0:NCTX], V(src.t[:, 8:8 + NCTX], src.res), cnt[:, 0:NCTX], ALU.mult)
                S.tt("dve", acc[c2][:, NCTX:NTOK], V(src.t[:, XO + 8 * 64:XO + 40 * 64], src.res), cnt[:, NCTX:NTOK],
                     ALU.mult)
                S.tt("pool", dfb[c2][:], acc[c2][:], u_[:], ALU.subtract)
            for dd in range(2):
                y_ = yb[dd]
                for bi, (t0, T) in enumerate(BLOCKS):
                    p_ = pps[bi % 2]
                    for c2 in range(2):
                        S.mm(p_[:, :T], pwt[:, c2, dd * 128:(dd + 1) * 128], dfb[c2][:, t0:t0 + T],
                             start=(c2 == 0), stop=(c2 == 1))
                    S.ts("dve", y_[:, t0:t0 + T], p_[:, :T], psc[:, gi_ * 2 + dd:gi_ * 2 + dd + 1], None, ALU.mult)
                row = 3072 + (gi_ * 2 + dd) * 128
                S.dma("act", V(K.yT[row:row + 128, :], K.yT_res[24 + gi_ * 2 + dd]), y_[:])


def phase_outproj(K, l):
    nc, S = K.nc, K.S
    from contextlib import ExitStack
    last = (l == DEPTH - 1)
    w_out = W(K, "w_out", l)
    with ExitStack() as st:
        regA = st.enter_context(sbt(nc, "regA", [128, 32 * 512], BF16))
        yTs = st.enter_context(sbt(nc, "yTs", [128, 32 * 512], BF16))
        L = Ctx()
        L.regA = regA
        L.resA = [Res(f"A{c}") for c in range(32)]
        resY = [Res(f"Y{c}") for c in range(32)]
        L.wp = [TT(st.enter_context(sbt(nc, f"wp{i}", [128, 32 * 512], BF16)), f"wp{i}") for i in range(2)]
        L.slab = [TT(st.enter_context(sbt(nc, f"slab{i}", [128, 256], F32)), f"slab{i}") for i in range(3)]
        L.oslab = [TT(st.enter_context(sbt(nc, f"oslab{i}", [128, 256], F32)), f"oslab{i}") for i in range(3)]
        L.py = [PT(st.enter_context(pst(nc, f"py{i}", [128, 512], F32)), f"py{i}") for i in range(2)]
        L.wi, L.yi, L.si = 0, 0, 0
        y3 = yTs[:, :].rearrange("p (c t) -> p c t", c=32)
        for bi, (t0, T) in enumerate(BLOCKS):
            if bi == 0 and last:
                continue
            r = 0 if bi == 0 else 1
            for c in range(32):
                S.dma("sp", V(y3[:, c, :T], resY[c]), V(K.yT[c * 128:(c + 1) * 128, t0:t0 + T], K.yT_res[c]))
            down_stage(K, L, l, 1, r, t0, T, y3, resY, 32, w_out, False, 0, bi == (1 if last else 0))


def phase_mixer(K, l):
    S = K.S
    parts = K.cfg.get("mix_parts", "igso")
    if "i" in parts:
        phase_inproj(K, l)
        S.barrier()
    if "g" in parts:
        phase_gdn(K, l)
        S.barrier()
    if "s" in parts:
        phase_sc_pool(K, l)
        S.barrier()
    if "o" in parts:
        phase_outproj(K, l)
        S.barrier()
```
